# Optimizing a Trainium2 kernel written in Bass

```python
import math
import jax, jax.numpy as jnp
from jax import lax
import numpy as np

D_MODEL = 1024
BATCH = 4
SEQ = 8192
DEPTH = 1

CHUNK = 64
Q_BLOCK = 128
EPS = 1e-6

MLA_HEADS = 8
QK_NOPE_DIM = 64
QK_ROPE_DIM = 32
V_HEAD_DIM = 128
Q_LORA_RANK = 384
KV_LORA_RANK = 256
MLA_WIDTH = MLA_HEADS * V_HEAD_DIM
ROPE_THETA = 10000.0

GMLP_GROUPS = 8
GMLP_GROUP_DIM = 128
GMLP_WIDTH = GMLP_GROUPS * GMLP_GROUP_DIM
SPATIAL_BLOCK = 128

D_MIX = MLA_WIDTH + GMLP_WIDTH

IN_SPLITS = (
    Q_LORA_RANK,
    KV_LORA_RANK,
    QK_ROPE_DIM,
    MLA_WIDTH,
    GMLP_WIDTH,
    GMLP_WIDTH,
    GMLP_WIDTH,
)
D_IN = sum(IN_SPLITS)

kernel_name = "hybrid_gmlp_mla_parallel_heads"


def rms_norm(x, g):
    xf = x.astype(jnp.float32)
    y = xf * lax.rsqrt(jnp.mean(xf * xf, axis=-1, keepdims=True) + EPS)
    return (y * g.astype(jnp.float32)).astype(x.dtype)


def layer_norm(x, g, b):
    xf = x.astype(jnp.float32)
    mu = jnp.mean(xf, axis=-1, keepdims=True)
    var = jnp.mean(jnp.square(xf - mu), axis=-1, keepdims=True)
    y = (xf - mu) * lax.rsqrt(var + EPS)
    return (y * g.astype(jnp.float32) + b.astype(jnp.float32)).astype(x.dtype)


def rope_tables(seq):
    pos = jnp.arange(seq, dtype=jnp.float32)
    inv_freq = ROPE_THETA ** (-jnp.arange(0, QK_ROPE_DIM, 2, dtype=jnp.float32) / QK_ROPE_DIM)
    ang = pos[:, None] * inv_freq[None, :]
    return jnp.cos(ang), jnp.sin(ang)


def apply_rope(x, cos, sin):
    xf = x.astype(jnp.float32)
    x1, x2 = jnp.split(xf, 2, axis=-1)
    out = jnp.concatenate([x1 * cos - x2 * sin, x1 * sin + x2 * cos], axis=-1)
    return out.astype(x.dtype)


def mla_branch(q_lat, kv_lat, k_rope_raw, q_norm_g, w_uq, kv_norm_g, w_ukv):
    b, s, _ = q_lat.shape
    cos, sin = rope_tables(s)
    q = jnp.einsum("bsr,rd->bsd", rms_norm(q_lat, q_norm_g), w_uq)
    q = q.reshape(b, s, MLA_HEADS, QK_NOPE_DIM + QK_ROPE_DIM)
    q_nope, q_rope = q[..., :QK_NOPE_DIM], q[..., QK_NOPE_DIM:]
    q_rope = apply_rope(q_rope, cos[None, :, None, :], sin[None, :, None, :])
    k_rope = apply_rope(k_rope_raw, cos[None], sin[None])
    kv = jnp.einsum("bsr,rd->bsd", rms_norm(kv_lat, kv_norm_g), w_ukv)
    kv = kv.reshape(b, s, MLA_HEADS, QK_NOPE_DIM + V_HEAD_DIM)
    k_nope, v = kv[..., :QK_NOPE_DIM], kv[..., QK_NOPE_DIM:]

    scale = 1.0 / math.sqrt(QK_NOPE_DIM + QK_ROPE_DIM)
    n_blocks = s // Q_BLOCK
    k_chunk = jnp.arange(s) // CHUNK
    qn_blocks = q_nope.reshape(b, n_blocks, Q_BLOCK, MLA_HEADS, QK_NOPE_DIM).transpose(1, 0, 2, 3, 4)
    qr_blocks = q_rope.reshape(b, n_blocks, Q_BLOCK, MLA_HEADS, QK_ROPE_DIM).transpose(1, 0, 2, 3, 4)
    starts = jnp.arange(n_blocks, dtype=jnp.int32) * Q_BLOCK

    def attend(args):
        qn, qr, start = args
        sc = (jnp.einsum("bqhd,bkhd->bhqk", qn, k_nope)
              + jnp.einsum("bqhr,bkr->bhqk", qr, k_rope)).astype(jnp.float32) * scale
        q_chunk = (start + jnp.arange(Q_BLOCK)) // CHUNK
        mask = k_chunk[None, :] <= q_chunk[:, None]
        sc = jnp.where(mask[None, None], sc, jnp.float32(-1e30))
        p = jax.nn.softmax(sc, axis=-1).astype(v.dtype)
        return jnp.einsum("bhqk,bkhd->bqhd", p, v)

    out = lax.map(attend, (qn_blocks, qr_blocks, starts))
    return out.transpose(1, 0, 2, 3, 4).reshape(b, s, MLA_WIDTH)


def gmlp_branch(u, v, ln_g, ln_b, w_spatial, b_spatial):
    b, s, _ = u.shape
    v = layer_norm(v, ln_g, ln_b)
    nc = s // SPATIAL_BLOCK
    v = v.reshape(b, nc, SPATIAL_BLOCK, GMLP_GROUPS, GMLP_GROUP_DIM)
    t_idx = jnp.arange(SPATIAL_BLOCK) // CHUNK
    mask = (t_idx[None, :] <= t_idx[:, None]).astype(w_spatial.dtype)
    ws = w_spatial * mask[None]
    mixed = jnp.einsum("gts,bcsgd->bctgd", ws, v) + b_spatial.T[None, None, :, :, None]
    return u * mixed.reshape(b, s, GMLP_WIDTH)


def setup_inputs(seed: int = 0) -> dict:
    key = jax.random.key(seed)
    ks = jax.random.split(key, 16)
    f32 = jnp.float32

    def normal(k, shape, scale):
        return jax.random.normal(k, shape, f32) * scale

    def gain(k, n):
        return 1.0 + 0.02 * jax.random.normal(k, (n,), f32)

    return {
        "x": jax.random.normal(ks[0], (BATCH, SEQ, D_MODEL), f32),
        "norm_in_g": gain(ks[1], D_MODEL),
        "w_in": normal(ks[2], (D_MODEL, D_IN), D_MODEL ** -0.5),
        "q_norm_g": gain(ks[3], Q_LORA_RANK),
        "w_uq": normal(ks[4], (Q_LORA_RANK, MLA_HEADS * (QK_NOPE_DIM + QK_ROPE_DIM)), Q_LORA_RANK ** -0.5),
        "kv_norm_g": gain(ks[5], KV_LORA_RANK),
        "w_ukv": normal(ks[6], (KV_LORA_RANK, MLA_HEADS * (QK_NOPE_DIM + V_HEAD_DIM)), KV_LORA_RANK ** -0.5),
        "gmlp_ln_g": gain(ks[7], GMLP_WIDTH),
        "gmlp_ln_b": normal(ks[8], (GMLP_WIDTH,), 0.02),
        "w_spatial": normal(ks[9], (GMLP_GROUPS, SPATIAL_BLOCK, SPATIAL_BLOCK), SPATIAL_BLOCK ** -0.5),
        "b_spatial": 1.0 + normal(ks[10], (GMLP_GROUPS, SPATIAL_BLOCK), 0.02),
        "out_norm_mla_g": gain(ks[11], MLA_WIDTH),
        "out_norm_gmlp_g": gain(ks[12], GMLP_WIDTH),
        "w_out": normal(ks[13], (D_MIX, D_MODEL), D_MIX ** -0.5),
        "final_norm_g": gain(ks[14], D_MODEL),
    }


def reference(x, norm_in_g, w_in, q_norm_g, w_uq, kv_norm_g, w_ukv,
              gmlp_ln_g, gmlp_ln_b, w_spatial, b_spatial,
              out_norm_mla_g, out_norm_gmlp_g, w_out, final_norm_g):
    h = x
    bounds = list(np.cumsum(IN_SPLITS)[:-1])
    for _ in range(DEPTH):
        y = rms_norm(h, norm_in_g)
        proj = jnp.einsum("bsd,de->bse", y, w_in)
        q_lat, kv_lat, k_rope, gate_mla, u, v, gate_gmlp = jnp.split(proj, bounds, axis=-1)

        o_mla = mla_branch(q_lat, kv_lat, k_rope, q_norm_g, w_uq, kv_norm_g, w_ukv)
        o_gmlp = gmlp_branch(jax.nn.gelu(u, approximate=False), jax.nn.gelu(v, approximate=False),
                             gmlp_ln_g, gmlp_ln_b, w_spatial, b_spatial)

        o_mla = rms_norm(o_mla, out_norm_mla_g) * jax.nn.silu(gate_mla)
        o_gmlp = rms_norm(o_gmlp, out_norm_gmlp_g) * jax.nn.silu(gate_gmlp)
        mixed = jnp.concatenate([o_mla, o_gmlp], axis=-1)
        h = h + jnp.einsum("bse,ed->bsd", mixed, w_out)
    return rms_norm(h, final_norm_g)
```

```python
import contextlib
import numpy as np
import concourse.bass as bass
import concourse.mybir as mybir
from concourse.bass_utils import run_bass_kernel_spmd

F32 = mybir.dt.float32
BF16 = mybir.dt.bfloat16
AF = mybir.ActivationFunctionType
ALU = mybir.AluOpType

D = 1024
NH = 8
EPS = 1e-6
SCALE = 1.0 / float(np.sqrt(96.0))
BIG = 30000.0
TILES_P = ((0, 3, 4, 7), (1, 2, 5, 6))
SFX = (0, 0, 1, 1, 2, 2, 3, 3)
N_WG = 9


class Tok:
    __slots__ = ("sem", "val", "eng")

    def __init__(self, sem, val, eng):
        self.sem, self.val, self.eng = sem, val, eng


class Res:
    __slots__ = ("name", "w", "r", "excl")

    def __init__(self, name, excl=False):
        self.name, self.w, self.r, self.excl = name, None, {}, excl


class Prog:
    ENG = ("pe", "act", "dve", "pool", "sp")
    ATTR = {"pe": "tensor", "act": "scalar", "dve": "vector", "pool": "gpsimd", "sp": "sync"}

    def __init__(self, nc, stack):
        self.nc, self.stack = nc, stack
        self.ops = {e: [] for e in self.ENG}
        self.sem = {e: stack.enter_context(nc.semaphore("s_" + e)) for e in self.ENG}
        self.cnt = {e: 0 for e in self.ENG}
        self.pending = {e: Tok(self.sem[e], None, e) for e in self.ENG}
        self.dsems = []
        self.dcount = {}
        self.dnext = 0
        self.ndma = 0

    def add_dma_sems(self, n):
        for i in range(n):
            s = self.stack.enter_context(self.nc.semaphore("dq%d" % len(self.dsems)))
            self.dsems.append(s)
            self.dcount[id(s)] = 0

    def _deps(self, r, w, extra):
        deps = []
        for x in r:
            if x.excl:
                w = list(w) + [x]
                continue
            if x.w is not None:
                deps.append(x.w)
        for x in w:
            if x.w is not None:
                deps.append(x.w)
            deps.extend(x.r.values())
        deps.extend(t for t in extra if t is not None)
        return deps, [x for x in r if not x.excl], w

    def op(self, eng, fn, r=(), w=(), sig=True, extra=()):
        deps, r, w = self._deps(r, w, extra)
        tok = self.pending[eng]
        for x in r:
            x.r[eng] = tok
        for x in w:
            x.w, x.r = tok, {}
        self.ops[eng].append([fn, deps, (self.sem[eng], 1) if sig else None])
        if sig:
            self.cnt[eng] += 1
            tok.val = self.cnt[eng]
            self.pending[eng] = Tok(self.sem[eng], None, eng)
        return tok

    def dma(self, eng, fn, r=(), w=(), extra=()):
        deps, r, w = self._deps(r, w, extra)
        s = self.dsems[self.dnext % len(self.dsems)]
        self.dnext += 1
        prev = self.dcount[id(s)]
        if prev:
            deps.append(Tok(s, prev, "dmaprev"))
        self.dcount[id(s)] = prev + 16
        self.ndma += 1
        tok = Tok(s, prev + 16, "dma%d" % self.ndma)
        for x in r:
            x.r[tok.eng] = tok
        for x in w:
            x.w, x.r = tok, {}
        self.ops[eng].append([fn, deps, (s, 16)])
        return tok

    def barrier(self):
        toks = [Tok(self.sem[e], self.cnt[e], "bar") for e in self.ENG if self.cnt[e] > 0]
        toks += [Tok(s, self.dcount[id(s)], "bar") for s in self.dsems if self.dcount[id(s)] > 0]
        for e in self.ENG:
            assert self.pending[e].val is None
            self.ops[e].append([None, list(toks), None])

    def wait_all(self, eng, toks):
        self.ops[eng].append([None, [t for t in toks if t is not None], None])

    def replay(self):
        nc = self.nc
        for e in self.ENG:
            assert all(True for _ in self.ops[e])
        with nc.Block() as block:
            for e in self.ENG:
                ops = self.ops[e]

                def body(engine, ops=ops, e=e):
                    waited = {}
                    for fn, deps, inc in ops:
                        for t in deps:
                            if t.eng == e and e == "pe":
                                continue
                            assert t.val is not None, "unresolved token"
                            k = id(t.sem)
                            if waited.get(k, -1) >= t.val:
                                continue
                            waited[k] = t.val
                            engine.wait_ge(t.sem, t.val)
                        if fn is not None:
                            ins = fn(engine)
                            if inc is not None:
                                ins.then_inc(inc[0], inc[1])
                getattr(block, self.ATTR[e])(body)


def build(NSB=8, dbg=False):
    S = NSB * 1024
    NBLK = S // 512
    NKT = S // 128
    NSLOT = NSB
    nc = bass.Bass("TRN2", target_bir_lowering=False)

    def din(name, shape, dt=F32):
        return nc.dram_tensor(name, list(shape), dt, kind="ExternalInput").ap()

    xk = din("xk", [S, D])
    xq = din("xq", [NSLOT * 512, D])
    ropek = din("ropek", [2, 32, S])
    ropeq = din("ropeq", [NSLOT, 32, 2, 512])
    kaug_d = din("kaug", [32, 1024])
    qaug_d = din("qaug", [32, 512])
    ident_d = din("ident", [128, 128])
    wkvr_d = din("w_kvr", [128, 8, 512])
    wmain_d = din("w_main", [N_WG, 128, 8, 512])
    gin_d = din("g_in", [128, 8])
    wqa_d = din("w_uqA", [128, 3, 1024])
    wqb_d = din("w_uqB", [128, 3, 1024])
    gq_d = din("g_q", [128, 3])
    wk_d = din("w_k", [128, 2, 1024])
    wv_d = din("w_v", [128, 2, 1024])
    gkv_d = din("g_kv", [128, 2])
    wst_d = din("wsT", [128, 8, 128])
    lb_d = din("ln_b", [1, 1024])
    bsp_d = din("bsp", [1, 1024])
    lg_d = din("lg", [128, 8])
    wout_d = din("w_out", [128, 16, 1024])
    gout_d = din("g_out", [128, 16])
    gfin_d = din("g_fin", [1, 1024])
    out_d = nc.dram_tensor("out", [NSLOT * 512, D], F32, kind="ExternalOutput").ap()

    wbf_d = nc.dram_tensor("wbf", [N_WG, 128, 4096], BF16, kind="Internal").ap()
    KT_d = nc.dram_tensor("KTs", [NH, 128, S], BF16, kind="Internal").ap()
    VS_d = nc.dram_tensor("VSs", [NH, 128, NKT * 130], BF16, kind="Internal").ap()

    dbg_out = {}
    if dbg:
        dbg_out["d_KT"] = nc.dram_tensor("d_KT", [NH, 128, S], BF16, kind="ExternalOutput").ap()
        dbg_out["d_VS"] = nc.dram_tensor("d_VS", [NH, 128, NKT * 130], BF16, kind="ExternalOutput").ap()
        dbg_out["d_QT"] = nc.dram_tensor("d_QT", [128, NH * 2 * 512], BF16, kind="ExternalOutput").ap()
        dbg_out["d_otok"] = nc.dram_tensor("d_otok", [128, 4 * 1024], BF16, kind="ExternalOutput").ap()
        dbg_out["d_mixT"] = nc.dram_tensor("d_mixT", [128, 16 * 512], BF16, kind="ExternalOutput").ap()

    with contextlib.ExitStack() as st:
        P = Prog(nc, st)
        P.add_dma_sems(40)

        def sb(name, shape, dt):
            return st.enter_context(nc.sbuf_tensor("sb_" + name, list(shape), dt))

        def bc_free(t, col, n):
            return bass.AP(t, col, [[t.shape[1], 128], [0, n]])

        psall = st.enter_context(nc.psum_tensor("psall", [128, 4096], F32))
        banks = [psall[:, i * 512:(i + 1) * 512] for i in range(8)]
        bres = [Res("bank%d" % i, excl=True) for i in range(8)]
        rot = {"i": 0}

        def nb(lst=(0, 1, 2, 3, 4, 5, 6)):
            b = lst[rot["i"] % len(lst)]
            rot["i"] += 1
            return b

        ident = sb("ident", [128, 128], BF16)
        ones = sb("ones", [128, 128], BF16)
        nhalf = sb("nhalf", [128, 8], F32)
        epst = sb("epst", [128, 1], F32)
        junk = sb("junk", [128, 1024], BF16)
        wqA = sb("wqA", [128, 3, 1024], BF16)
        wqB = sb("wqB", [128, 3, 1024], BF16)
        wsT = sb("wsTb", [128, 8, 128], BF16)
        B2 = sb("B2", [128, 8, 128], F32)
        lg = sb("lg", [128, 8], F32)
        wout = sb("woutb", [128, 16, 1024], BF16)
        gfin = sb("gfin", [128, 1024], F32)
        kaug = sb("kaugb", [32, 1024], BF16)
        qaug = sb("qaugb", [32, 512], BF16)
        gin = sb("gin", [128, 8], F32)
        gq = sb("gq", [128, 3], F32)
        gkv = sb("gkv", [128, 2], F32)
        gout = sb("gout", [128, 16], F32)
        r_const = Res("const")
        r_w = Res("weights")
        r_junk = Res("junk")
        arena = sb("arena", [128, 8192], BF16)
        r_arena = [Res("arena0"), Res("arena1")]

        def arena_view(off, a, b_):
            return arena[:, off:off + a * b_].rearrange("p (a b) -> p a b", b=b_)

        wkvr = arena_view(0, 8, 512)
        wk = arena_view(4096, 2, 1024)
        wv = arena_view(6144, 2, 1024)
        wring = [arena_view(0, 8, 512), arena_view(4096, 8, 512)]
        xs = [sb("xs%d" % i, [128, 1024], F32) for i in range(2)]
        r_xs = [Res("xs%d" % i) for i in range(2)]
        yb = [sb("yb%d" % i, [128, 1024], BF16) for i in range(2)]
        r_yb = [Res("yb%d" % i) for i in range(2)]
        yT = sb("yT", [128, 8, 512], BF16)
        r_yT = Res("yT")
        ssq = sb("ssq", [128, 8], F32)
        rstd = sb("rstd", [128, 8], F32)
        r_ssq = [Res("ssq%d" % i) for i in range(8)]
        r_rstd = [Res("rstd%d" % i) for i in range(8)]
        r_wbf = Res("wbf")

        with contextlib.ExitStack() as st0:
            NSTG = 4
            stage = [st0.enter_context(nc.sbuf_tensor("stage%d" % i, [128, 4096], F32)) for i in range(NSTG)]
            r_stage = [Res("stage%d" % i) for i in range(NSTG)]
            wtmp = [st0.enter_context(nc.sbuf_tensor("wtmp%d" % i, [128, 4096], BF16)) for i in range(2)]
            r_wtmp = [Res("wtmp%d" % i) for i in range(2)]
            lbb = st0.enter_context(nc.sbuf_tensor("lbb", [128, 1024], BF16))
            r_lbb = Res("lbb")
            P.dma("sp", lambda e: e.dma_start(out=stage[0][:, 0:128], in_=ident_d), w=[r_stage[0]])
            P.op("dve", lambda e: e.tensor_copy(out=ident[:], in_=stage[0][:, 0:128]), r=[r_stage[0]], w=[r_const])
            P.op("pool", lambda e: e.memset(ones[:], 1.0), w=[r_const])
            P.op("pool", lambda e: e.memset(nhalf[:], -0.5), w=[r_const])
            P.op("pool", lambda e: e.memset(epst[:], EPS), w=[r_const])
            for (dst, src) in ((gin, gin_d), (gq, gq_d), (gkv, gkv_d), (gout, gout_d), (lg, lg_d)):
                P.dma("sp", lambda e, dst=dst, src=src: e.dma_start(out=dst[:], in_=src), w=[r_const])
            P.dma("sp", lambda e: e.dma_start(out=gfin[:], in_=bass.AP(gfin_d.tensor, 0, [[0, 128], [1, 1024]])), w=[r_const])
            P.dma("sp", lambda e: e.dma_start(out=stage[1][0:32, 0:1024], in_=kaug_d), w=[r_stage[1]])
            P.dma("sp", lambda e: e.dma_start(out=stage[1][0:32, 1024:1536], in_=qaug_d), w=[r_stage[1]])
            P.op("dve", lambda e: e.tensor_copy(out=kaug[:], in_=stage[1][0:32, 0:1024]), r=[r_stage[1]], w=[r_const])
            P.op("dve", lambda e: e.tensor_copy(out=qaug[:], in_=stage[1][0:32, 1024:1536]), r=[r_stage[1]], w=[r_const])
            sidx = {"i": 0}

            def load_scale_cast(src_ap, ncols, nk, g_tile, g_off, dst_ap, eng="dve", wres=None):
                i = sidx["i"] % NSTG
                sidx["i"] += 1
                stg = stage[i][:, 0:nk * ncols].rearrange("p (k c) -> p k c", c=ncols)
                P.dma("sp", lambda e: e.dma_start(out=stg, in_=src_ap), w=[r_stage[i]])
                gb = bass.AP(g_tile, g_off, [[g_tile.shape[1], 128], [1, nk], [0, ncols]])
                return P.op(eng, lambda e: e.tensor_tensor(out=dst_ap, in0=stg, in1=gb, op=ALU.mult),
                            r=[r_stage[i], r_const], w=[wres if wres is not None else r_w])

            load_scale_cast(wkvr_d, 512, 8, gin, 0, wkvr)
            P.op("dve", lambda e: e.tensor_scalar(out=wkvr[:, :, 448:464], in0=wkvr[:, :, 448:464], scalar1=-1.0,
                                                   scalar2=None, op0=ALU.mult), r=[r_w], w=[r_w])
            load_scale_cast(wk_d, 1024, 2, gkv, 0, wk)
            load_scale_cast(wv_d, 1024, 2, gkv, 0, wv)
            load_scale_cast(wqa_d, 1024, 3, gq, 0, wqA[:])
            load_scale_cast(wqb_d, 1024, 3, gq, 0, wqB[:])
            wqB4 = wqB[:].rearrange("p k (h c) -> p k h c", c=128)
            for kk in range(3):
                P.op("dve", lambda e, kk=kk: e.tensor_scalar(out=wqB4[:, kk, :, 64:80], in0=wqB4[:, kk, :, 64:80],
                                                              scalar1=-1.0, scalar2=None, op0=ALU.mult), r=[r_w], w=[r_w])
            for q in range(4):
                load_scale_cast(wout_d[:, q * 4:(q + 1) * 4, :], 1024, 4, gout, q * 4, wout[:, q * 4:(q + 1) * 4, :])
            def wm_load(gi):
                i = sidx["i"] % NSTG
                sidx["i"] += 1
                stg = stage[i][:, 0:4096].rearrange("p (k c) -> p k c", c=512)
                P.dma("sp", lambda e, stg=stg, gi=gi: e.dma_start(out=stg, in_=wmain_d[gi]), w=[r_stage[i]])
                return i, stg

            wml = {gi: wm_load(gi) for gi in range(min(NSTG - 1, N_WG))}
            for gi in range(N_WG):
                i, stg = wml.pop(gi)
                wt3 = wtmp[gi % 2][:].rearrange("p (k c) -> p k c", c=512)
                gb = bass.AP(gin, 0, [[8, 128], [1, 8], [0, 512]])
                P.op("dve" if gi % 3 != 2 else "pool", lambda e, stg=stg, gb=gb, wt3=wt3: e.tensor_tensor(out=wt3, in0=stg, in1=gb, op=ALU.mult),
                     r=[r_stage[i], r_const], w=[r_wtmp[gi % 2]])
                if gi + NSTG - 1 < N_WG:
                    wml[gi + NSTG - 1] = wm_load(gi + NSTG - 1)
                P.dma("sp", lambda e, gi=gi: e.dma_start(out=wbf_d[gi], in_=wtmp[gi % 2][:]), r=[r_wtmp[gi % 2]], w=[r_wbf])
            i = sidx["i"] % NSTG
            sidx["i"] += 1
            stg = stage[i][:, 0:1024].rearrange("p (g t) -> p g t", t=128)
            P.dma("sp", lambda e: e.dma_start(out=stg, in_=wst_d), w=[r_stage[i]])
            P.op("dve", lambda e: e.tensor_copy(out=wsT[:], in_=stg), r=[r_stage[i]], w=[r_w])
            P.op("dve", lambda e: e.memset(wsT[64:128, :, 0:64], 0.0), w=[r_w])
            i2 = sidx["i"] % NSTG
            sidx["i"] += 1
            lbst = stage[i2]
            P.dma("sp", lambda e: e.dma_start(out=lbst[:, 0:1024], in_=bass.AP(lb_d.tensor, 0, [[0, 128], [1, 1024]])), w=[r_stage[i2]])
            P.dma("sp", lambda e: e.dma_start(out=lbst[:, 1024:2048], in_=bass.AP(bsp_d.tensor, 0, [[0, 128], [1, 1024]])), w=[r_stage[i2]])
            P.op("dve", lambda e: e.tensor_copy(out=lbb[:], in_=lbst[:, 0:1024]), r=[r_stage[i2]], w=[r_lbb])
            for half in range(2):
                b = nb()
                for g4 in range(4):
                    g = half * 4 + g4
                    P.op("pe", lambda e, b=b, g=g, g4=g4: e.matmul(banks[b][:, g4 * 128:(g4 + 1) * 128],
                                                                   lhsT=lbb[:, g * 128:(g + 1) * 128], rhs=wsT[:, g, :],
                                                                   start=True, stop=True),
                         r=[r_lbb, r_w], w=[bres[b]], sig=(g4 == 3))
                P.op("dve", lambda e, b=b, half=half: e.tensor_tensor(
                    out=B2[:, half * 4:(half + 1) * 4, :].rearrange("p g t -> p (g t)"), in0=banks[b][:],
                    in1=lbst[:, 1024 + half * 512:1024 + (half + 1) * 512], op=ALU.add),
                    r=[bres[b], r_stage[i2]], w=[r_const])
            P.barrier()

        xcnt = {"i": 0}
        out_toks = []

        def rsqrt_col(src_ap, dst_ap, n, r_src, r_dst, inv_n, eng_a="dve"):
            P.op(eng_a, lambda e: e.tensor_scalar(out=dst_ap, in0=src_ap, scalar1=inv_n, scalar2=EPS,
                                                  op0=ALU.mult, op1=ALU.add), r=r_src, w=r_dst)
            return P.op("pool", lambda e: e.tensor_tensor(out=dst_ap, in0=dst_ap, in1=nhalf[:, 0:n], op=ALU.pow),
                        r=[r_const], w=r_dst)

        def rsqrt_big(src_ap, dst_ap, r_src, r_dst, inv_n):
            P.op("act", lambda e: e.activation(out=dst_ap, in_=src_ap, func=AF.Ln, scale=inv_n, bias=epst[:, 0:1]),
                 r=r_src + [r_const], w=r_dst)
            return P.op("act", lambda e: e.activation(out=dst_ap, in_=dst_ap, func=AF.Exp, scale=-0.5), w=r_dst)

        xring = {"bufs": list(xs), "res": list(r_xs)}

        def x_loads(src_rows, subs=(0, 1, 2, 3)):
            idx = {}
            for i in subs:
                c = xcnt["i"]
                xcnt["i"] += 1
                xi = c % len(xring["bufs"])
                xb_, xr_ = xring["bufs"][xi], xring["res"][xi]
                P.dma("sp", lambda e, xb_=xb_, i=i: e.dma_start(out=xb_[:], in_=src_rows(i)), w=[xr_])
                idx[i] = (c, xb_, xr_)
            return idx

        def norm_transpose(src_rows, yT=yT, r_yT=r_yT):
            for i in range(4):
                norm_compute(x_loads(src_rows, (i,)), yT, r_yT)

        def norm_compute(idx, yT=yT, r_yT=r_yT):
            for i in sorted(idx):
                c, xb_, xr_ = idx[i]
                xi, si = c % 2, c % 8
                P.op("act", lambda e, xb_=xb_, si=si: e.activation(out=junk[:], in_=xb_[:], func=AF.Square,
                                                                    accum_out=ssq[:, si:si + 1]),
                     r=[xr_], w=[r_junk, r_ssq[si]])
                rsqrt_col(ssq[:, si:si + 1], rstd[:, si:si + 1], 1, [r_ssq[si]], [r_rstd[si]], 1.0 / D, eng_a="pool")
                P.op("dve", lambda e, xi=xi, si=si, xb_=xb_: e.tensor_scalar(out=yb[xi][:], in0=xb_[:], scalar1=rstd[:, si:si + 1],
                                                                              scalar2=None, op0=ALU.mult),
                     r=[xr_, r_rstd[si]], w=[r_yb[xi]])
                b = (6, 7)[c % 2]
                pt = banks[b][:].bitcast(BF16)
                for kk in range(8):
                    P.op("pe", lambda e, pt=pt, kk=kk, xi=xi: e.transpose(out=pt[:, kk * 128:(kk + 1) * 128],
                                                                         in_=yb[xi][:, kk * 128:(kk + 1) * 128], identity=ident[:]),
                         r=[r_yb[xi], r_const], w=[bres[b]], sig=(kk == 7))
                if i % 2 == 0:
                    P.op("dve", lambda e, pt=pt, i=i: e.tensor_copy(out=yT[:, :, i * 128:(i + 1) * 128],
                                                                   in_=pt.rearrange("p (k t) -> p k t", t=128)),
                         r=[bres[b]], w=[r_yT])
                else:
                    P.op("act", lambda e, pt=pt, i=i: e.activation(out=yT[:, :, i * 128:(i + 1) * 128],
                                                                  in_=pt.rearrange("p (k t) -> p k t", t=128), func=AF.Copy),
                         r=[bres[b]], w=[r_yT])

        B6 = (0, 1, 2, 3, 4, 5)
        with contextlib.ExitStack() as st1:
            def sb1(name, shape, dt):
                return st1.enter_context(nc.sbuf_tensor("sb1_" + name, list(shape), dt))

            kvl = sb1("kvl", [128, 2, 512], BF16)
            kvsq = sb1("kvsq", [128, 2, 512], BF16)
            rkv = sb1("rkv", [128, 512], F32)
            rkc = sb1("rkc", [128, 4], F32)
            rtab = [sb1("rtab%d" % i, [128, 2, 512], F32) for i in range(2)]
            krope = [sb1("krope%d" % i, [128, 512], BF16) for i in range(2)]
            r_krope = [Res("krope0"), Res("krope1")]
            rt1 = sb1("rt1", [128, 512], F32)
            rt2 = sb1("rt2", [128, 512], F32)
            kst = [sb1("kst%d" % i, [128, 8, 512], BF16) for i in range(2)]
            vst = [sb1("vst%d" % i, [128, 8, 4, 130], BF16) for i in range(2)]
            r_kvl, r_kvsq, r_rkv, r_rkc = Res("kvl"), Res("kvsq"), Res("rkv"), Res("rkc")
            r_rtab = [Res("rtab0"), Res("rtab1")]
            r_rt1, r_rt2 = Res("rt1"), Res("rt2")
            r_kst = [Res("kst0"), Res("kst1")]
            r_vst = [Res("vst0"), Res("vst1")]
            r_KT, r_VS = Res("KT"), Res("VS")
            for i in range(2):
                P.op("pool", lambda e, i=i: e.memset(vst[i][:], 1.0), w=[r_vst[i]])

            for sbi in range(NSB):
                P.dma("sp", lambda e, sbi=sbi: e.dma_start(
                    out=KT_d.rearrange("h p s -> p h s")[96:128, :, sbi * 1024:(sbi + 1) * 1024],
                    in_=bass.AP(kaug, 0, [[1024, 32], [0, 8], [1, 1024]])), r=[r_const], w=[r_KT])
            yT2 = sb1("yT2", [128, 8, 512], BF16)
            yTs = [(yT, r_yT), (yT2, Res("yT2"))]

            xs1 = [sb1("xs1_%d" % i, [128, 1024], F32) for i in range(2)]
            xring["bufs"] = list(xs) + xs1
            xring["res"] = list(r_xs) + [Res("xs1_0"), Res("xs1_1")]
            xl = {}

            def L1(blk):
                xl[blk] = x_loads(lambda i, blk=blk: xk[(blk * 4 + i) * 128:(blk * 4 + i + 1) * 128, :])

            def C1(blk):
                norm_compute(xl.pop(blk), yT=yTs[blk % 2][0], r_yT=yTs[blk % 2][1])

            L1(0)
            C1(0)
            if NBLK > 1:
                L1(1)
            for blk in range(NBLK):
                pb_ = blk % 2
                if blk + 1 < NBLK:
                    C1(blk + 1)
                if blk + 2 < NBLK:
                    L1(blk + 2)
                yTc, r_yTc = yTs[blk % 2]
                P.dma("sp", lambda e, blk=blk, pb_=pb_: e.dma_start(
                    out=rtab[pb_][64:96, :, :], in_=ropek.rearrange("t r s -> r t s")[:, :, blk * 512:(blk + 1) * 512]),
                    w=[r_rtab[pb_]])
                bk = [nb(B6) for _ in range(4)]
                for c4 in range(4):
                    for kk in range(8):
                        P.op("pe", lambda e, c4=c4, kk=kk, b=bk[c4], yTc=yTc: e.matmul(banks[b][:], lhsT=wkvr[:, kk, c4 * 128:(c4 + 1) * 128],
                                                                             rhs=yTc[:, kk, :], start=(kk == 0), stop=(kk == 7)),
                             r=[r_w, r_yTc], w=[bres[bk[c4]]], sig=(kk == 7))
                for c in range(2):
                    P.op("act", lambda e, c=c, b=bk[c]: e.activation(out=kvl[:, c, :], in_=banks[b][:], func=AF.Copy),
                         r=[bres[bk[c]]], w=[r_kvl])
                    P.op("act", lambda e, c=c, b=bk[c]: e.activation(out=kvsq[:, c, :], in_=banks[b][:], func=AF.Square),
                         r=[bres[bk[c]]], w=[r_kvsq])
                P.op("dve", lambda e, b=bk[2], pb_=pb_: e.tensor_tensor(out=rt1[64:96, :], in0=banks[b][64:96, :],
                                                                       in1=rtab[pb_][64:96, 0, :], op=ALU.mult),
                     r=[bres[bk[2]], r_rtab[pb_]], w=[r_rt1])
                P.op("dve", lambda e, b=bk[3], pb_=pb_: e.tensor_tensor(out=rt2[64:96, :], in0=banks[b][64:96, :],
                                                                       in1=rtab[pb_][64:96, 1, :], op=ALU.mult),
                     r=[bres[bk[3]], r_rtab[pb_]], w=[r_rt2])
                P.op("dve", lambda e, pb_=pb_: e.tensor_tensor(out=krope[pb_][64:96, :], in0=rt1[64:96, :], in1=rt2[64:96, :],
                                                               op=ALU.add), r=[r_rt1, r_rt2], w=[r_krope[pb_]])
                P.dma("sp", lambda e, pb_=pb_, blk=blk: e.dma_start(
                    out=KT_d.rearrange("h p s -> p h s")[64:96, :, blk * 512:(blk + 1) * 512],
                    in_=bass.AP(krope[pb_], 64 * 512, [[512, 32], [0, 8], [1, 512]])), r=[r_krope[pb_]], w=[r_KT])
                b1 = nb(B6)
                for c in range(2):
                    P.op("pe", lambda e, c=c, b1=b1: e.matmul(banks[b1][:], lhsT=ones[:], rhs=kvsq[:, c, :],
                                                             start=(c == 0), stop=(c == 1)),
                         r=[r_kvsq, r_const], w=[bres[b1]], sig=(c == 1))
                rsqrt_big(banks[b1][:], rkv[:], [bres[b1]], [r_rkv], 1.0 / 256)
                b2 = nb(B6)
                for i in range(4):
                    for c in range(2):
                        P.op("pe", lambda e, c=c, i=i, b2=b2: e.matmul(banks[b2][:, i:i + 1], lhsT=kvsq[:, c, i * 128:(i + 1) * 128],
                                                                      rhs=ones[:, 0:1], start=(c == 0), stop=(c == 1)),
                             r=[r_kvsq, r_const], w=[bres[b2]], sig=(i == 3 and c == 1))
                rsqrt_col(banks[b2][:, 0:4], rkc[:], 4, [bres[b2]], [r_rkc], 1.0 / 256)
                for h in range(NH):
                    b = nb(B6)
                    for c in range(2):
                        P.op("pe", lambda e, c=c, h=h, b=b: e.matmul(banks[b][:], lhsT=wk[:, c, h * 128:(h + 1) * 128],
                                                                    rhs=kvl[:, c, :], start=(c == 0), stop=(c == 1)),
                             r=[r_w, r_kvl], w=[bres[b]], sig=(c == 1))
                    P.op("dve", lambda e, h=h, b=b, pb_=pb_: e.tensor_tensor(out=kst[pb_][0:64, h, :], in0=banks[b][0:64, :],
                                                                            in1=rkv[0:64, :], op=ALU.mult),
                         r=[bres[b], r_rkv], w=[r_kst[pb_]])
                P.dma("sp", lambda e, pb_=pb_, blk=blk: e.dma_start(
                    out=KT_d.rearrange("h p s -> p h s")[0:64, :, blk * 512:(blk + 1) * 512], in_=kst[pb_][0:64, :, :]),
                    r=[r_kst[pb_]], w=[r_KT])
                for i in range(4):
                    for half in range(2):
                        b = nb(B6)
                        for c in range(2):
                            P.op("pe", lambda e, c=c, i=i, half=half, b=b: e.matmul(
                                banks[b][:], lhsT=kvl[:, c, i * 128:(i + 1) * 128], rhs=wv[:, c, half * 512:(half + 1) * 512],
                                start=(c == 0), stop=(c == 1)), r=[r_kvl, r_w], w=[bres[b]], sig=(c == 1))
                        if half == 0:
                            P.op("dve", lambda e, i=i, half=half, b=b, pb_=pb_: e.tensor_scalar(
                                out=vst[pb_][:, half * 4:(half + 1) * 4, i, 0:128],
                                in0=banks[b][:].rearrange("p (h c) -> p h c", c=128), scalar1=rkc[:, i:i + 1], scalar2=None,
                                op0=ALU.mult), r=[bres[b], r_rkc], w=[r_vst[pb_]])
                        else:
                            P.op("act", lambda e, i=i, half=half, b=b, pb_=pb_: e.activation(
                                out=vst[pb_][:, half * 4:(half + 1) * 4, i, 0:128],
                                in_=banks[b][:].rearrange("p (h c) -> p h c", c=128), func=AF.Identity, scale=rkc[:, i:i + 1]),
                                r=[bres[b], r_rkc], w=[r_vst[pb_]])
                P.dma("sp", lambda e, pb_=pb_, blk=blk: e.dma_start(
                    out=VS_d.rearrange("h p (t c) -> p h t c", c=130)[:, :, blk * 4:(blk + 1) * 4, :], in_=vst[pb_][:]),
                    r=[r_vst[pb_]], w=[r_VS])
            P.barrier()
            xring["bufs"] = list(xs)
            xring["res"] = list(r_xs)
            xcnt["i"] = 0
            if dbg:
                out_toks.append(P.dma("sp", lambda e: e.dma_start(out=dbg_out["d_KT"], in_=KT_d)))
                out_toks.append(P.dma("sp", lambda e: e.dma_start(out=dbg_out["d_VS"], in_=VS_d)))

        qlT = sb("qlT", [128, 3, 512], BF16)
        sqq = sb("sqq", [128, 3, 512], BF16)
        sgm = sb("sgm", [128, 8, 512], BF16)
        uT = sb("uT", [128, 8, 512], BF16)
        otok = uT[:].rearrange("p k t -> p (k t)").rearrange("p (i f) -> p i f", f=1024)
        sgg = sb("sgg", [128, 8, 512], BF16)
        sggf = sgg[:].rearrange("p k t -> p (k t)")
        hb = [sggf[:, i * 2048:(i + 1) * 2048].bitcast(F32) for i in range(2)]
        big16 = sb("big16", [128, 8192], BF16)
        QT = big16[:, 0:4096].rearrange("p (h s) -> p h s", s=512)
        vf = [big16[:, i * 2048:(i + 1) * 2048].bitcast(F32) for i in range(2)]
        vhat = big16[:, 4096:8192].rearrange("p (i f) -> p i f", f=1024)
        mixT = sb("mixT", [128, 16, 512], BF16)
        tmpg = [sb("tmpg%d" % i, [128, 512], F32) for i in range(2)]
        og = [sb("og%d" % i, [128, 512], BF16) for i in range(2)]
        sqg = yT
        rtq = sb("rtq", [128, 2, 512], F32)
        CR = sb("CR", [128, 512], F32)
        SR = sb("SR", [128, 512], F32)
        t2 = [sb("t2_%d" % i, [128, 512], F32) for i in range(2)]
        NKB = 3
        kvK = [sb("kvK%d" % i, [128, 1024], BF16) for i in range(NKB)]
        kvV = [sb("kvV%d" % i, [128, 8, 130], BF16) for i in range(NKB)]
        NPB = 3
        pTs = [sb("pTs%d" % i, [128, 2, 512], BF16) for i in range(NPB)]
        rl = sb("rl", [128, 8], F32)
        bst = sb("bst", [128, 12], F32)
        mv = sb("mv", [128, 4], F32)
        r12 = sb("r12", [128, 16], F32)
        (r_qlT, r_sqq, r_rq, r_sgm, r_uT, r_sgg, r_mixT, r_rtq, r_CR, r_SR, r_rl, r_bst,
         r_mv) = [Res(n) for n in ("qlT", "sqq", "rq", "sgm", "uT", "sgg", "mixT", "rtq", "CR", "SR", "rl", "bst", "mv")]
        r_otok = r_uT
        r_hb = [r_sgg, r_sgg]
        r_sqg = r_yT
        r_tmpg = [Res("tmpg0"), Res("tmpg1")]
        r_og = [Res("og0"), Res("og1")]
        r_t2 = [Res("t2a"), Res("t2b")]
        r_kv = [Res("kv%d" % i) for i in range(NKB)]
        r_pT = [Res("pT%d" % i) for i in range(NPB)]
        r_r1 = [Res("r1_%d" % i) for i in range(4)]
        r_r2 = Res("r2")
        r_rf = [Res("rf%d" % i) for i in range(4)]
        r_QTh = [Res("QT%d" % h) for h in range(NH)]
        r_out = Res("out")
        wcnt = {"i": 0}
        itc = {"i": 0}
        kvc = {"i": 0}
        WG_ORDER = (0, 1, 2, 5, 6, 3, 4, 7, 8)

        def wload(gi):
            ri = wcnt["i"] % 2
            wcnt["i"] += 1
            P.dma("sp", lambda e, ri=ri, gi=gi: e.dma_start(out=wring[ri], in_=wbf_d[gi].rearrange("p (k c) -> p k c", c=512)),
                  w=[r_arena[ri]])
            return ri

        def stageA(j):
            norm_transpose(lambda i, j=j: xq[(j * 4 + i) * 128:(j * 4 + i + 1) * 128, :])

        def stageB_items(j):
            items = []

            def grp(gi):
                ri = wload(gi)
                nct = 3 if gi == 0 else 4
                for ct in range(nct):
                    b = nb()
                    for kk in range(8):
                        P.op("pe", lambda e, ri=ri, ct=ct, kk=kk, b=b: e.matmul(banks[b][:], lhsT=wring[ri][:, kk, ct * 128:(ct + 1) * 128],
                                                                               rhs=yT[:, kk, :], start=(kk == 0), stop=(kk == 7)),
                             r=[r_arena[ri], r_yT], w=[bres[b]], sig=(kk == 7))
                    if gi == 0:
                        P.op("act", lambda e, ct=ct, b=b: e.activation(out=qlT[:, ct, :], in_=banks[b][:], func=AF.Copy),
                             r=[bres[b]], w=[r_qlT])
                        P.op("act", lambda e, ct=ct, b=b: e.activation(out=sqq[:, ct, :], in_=banks[b][:], func=AF.Square),
                             r=[bres[b]], w=[r_sqq])
                    else:
                        dst, rr, fn = {1: (sgm, r_sgm, AF.Silu), 2: (sgm, r_sgm, AF.Silu), 5: (sgg, r_sgg, AF.Silu),
                                       6: (sgg, r_sgg, AF.Silu), 3: (uT, r_uT, AF.Gelu), 4: (uT, r_uT, AF.Gelu)}[gi]
                        t = ((gi - 1) % 2) * 4 + ct
                        P.op("act", lambda e, dst=dst, t=t, b=b, fn=fn: e.activation(out=dst[:, t, :], in_=banks[b][:], func=fn),
                             r=[bres[b]], w=[rr])

            for gi in WG_ORDER[:7]:
                items.append(lambda gi=gi: grp(gi))
            rv = []

            def vpart(i):
                if i == 0:
                    rv.extend([wload(7), wload(8)])
                vi = i % 2
                for half in range(2):
                    b = nb()
                    for kk in range(8):
                        P.op("pe", lambda e, i=i, half=half, kk=kk, b=b: e.matmul(
                            banks[b][:], lhsT=yT[:, kk, i * 128:(i + 1) * 128], rhs=wring[rv[half]][:, kk, :],
                            start=(kk == 0), stop=(kk == 7)), r=[r_arena[rv[half]], r_yT], w=[bres[b]], sig=(kk == 7))
                    P.op("act", lambda e, vi=vi, half=half, b=b: e.activation(out=vf[vi][:, half * 512:(half + 1) * 512],
                                                                               in_=banks[b][:], func=AF.Gelu),
                         r=[bres[b]], w=r_QTh)
                for half in range(2):
                    P.op("dve", lambda e, vi=vi, half=half: e.bn_stats(out=bst[:, half * 6:(half + 1) * 6],
                                                                       in_=vf[vi][:, half * 512:(half + 1) * 512]),
                         r=r_QTh, w=[r_bst])
                P.op("dve", lambda e: e.bn_aggr(out=mv[:, 0:2], in_=bst[:, 0:12]), r=[r_bst], w=[r_mv])
                rsqrt_col(mv[:, 1:2], mv[:, 2:3], 1, [r_mv], [r_mv], 1.0)
                P.op("dve", lambda e, vi=vi, i=i: e.tensor_scalar(out=vhat[:, i, :], in0=vf[vi][:], scalar1=mv[:, 0:1],
                                                                   scalar2=mv[:, 2:3], op0=ALU.subtract, op1=ALU.mult),
                     r=[r_mv], w=r_QTh)

            for i in range(4):
                items.append(lambda i=i: vpart(i))
            return items

        def stageCD(j):
            P.dma("sp", lambda e, j=j: e.dma_start(out=rtq[64:96, :, :], in_=ropeq[j]), w=[r_rtq])
            for g in range(8):
                b = nb()
                gi2 = g % 2
                for i in range(4):
                    P.op("pe", lambda e, g=g, i=i, b=b: e.matmul(banks[b][:, i * 128:(i + 1) * 128],
                                                                lhsT=vhat[:, i, g * 128:(g + 1) * 128], rhs=wsT[:, g, :],
                                                                start=True, stop=True),
                         r=r_QTh + [r_w], w=[bres[b]], sig=(i == 3))
                P.op("dve", lambda e, g=g, b=b, gi2=gi2: e.scalar_tensor_tensor(
                    out=tmpg[gi2][:], in0=banks[b][:], scalar=lg[:, g:g + 1],
                    in1=bass.AP(B2, g * 128, [[1024, 128], [0, 4], [1, 128]]), op0=ALU.mult, op1=ALU.add),
                    r=[bres[b], r_const], w=[r_tmpg[gi2]])
                P.op("dve", lambda e, g=g, gi2=gi2: e.tensor_tensor(out=og[gi2][:], in0=tmpg[gi2][:], in1=uT[:, g, :], op=ALU.mult),
                     r=[r_tmpg[gi2], r_uT], w=[r_og[gi2]])
                P.op("pool", lambda e, g=g, gi2=gi2: e.tensor_tensor(out=mixT[:, 8 + g, :], in0=og[gi2][:], in1=sgg[:, g, :], op=ALU.mult),
                     r=[r_og[gi2], r_sgg], w=[r_mixT])
                P.op("act", lambda e, g=g, gi2=gi2: e.activation(out=sqg[:, g, :], in_=og[gi2][:], func=AF.Square),
                     r=[r_og[gi2]], w=[r_sqg])
            b = nb()
            for i in range(4):
                for g in range(8):
                    P.op("pe", lambda e, g=g, i=i, b=b: e.matmul(banks[b][:, i:i + 1], lhsT=sqg[:, g, i * 128:(i + 1) * 128],
                                                                rhs=ones[:, 0:1], start=(g == 0), stop=(g == 7)),
                         r=[r_sqg, r_const], w=[bres[b]], sig=(i == 3 and g == 7))
            rsqrt_col(banks[b][:, 0:4], r12[:, 4:8], 4, [bres[b]], [r_r2], 1.0 / 1024)
            b = nb()
            for c in range(3):
                P.op("pe", lambda e, c=c, b=b: e.matmul(banks[b][:], lhsT=ones[:], rhs=sqq[:, c, :], start=(c == 0), stop=(c == 2)),
                     r=[r_sqq, r_const], w=[bres[b]], sig=(c == 2))
            rsqrt_big(banks[b][:], CR[:], [bres[b]], [r_CR], 1.0 / 384)
            P.op("dve", lambda e: e.tensor_tensor(out=SR[64:96, :], in0=rtq[64:96, 1, :], in1=CR[64:96, :], op=ALU.mult),
                 r=[r_CR, r_rtq], w=[r_SR])
            P.op("dve", lambda e: e.tensor_tensor(out=CR[64:96, :], in0=rtq[64:96, 0, :], in1=CR[64:96, :], op=ALU.mult),
                 r=[r_rtq, r_SR], w=[r_CR])
            P.dma("sp", lambda e: e.dma_start(out=QT[96:128, :, :], in_=bass.AP(qaug, 0, [[512, 32], [0, 8], [1, 512]])),
                  r=[r_const], w=r_QTh)
            for h in range(NH):
                bA, bB = nb(), nb()
                for c in range(3):
                    P.op("pe", lambda e, c=c, h=h, bA=bA: e.matmul(banks[bA][:], lhsT=wqA[:, c, h * 128:(h + 1) * 128], rhs=qlT[:, c, :],
                                                                  start=(c == 0), stop=(c == 2)),
                         r=[r_w, r_qlT], w=[bres[bA]], sig=(c == 2))
                for c in range(3):
                    P.op("pe", lambda e, c=c, h=h, bB=bB: e.matmul(banks[bB][:], lhsT=wqB[:, c, h * 128:(h + 1) * 128], rhs=qlT[:, c, :],
                                                                  start=(c == 0), stop=(c == 2)),
                         r=[r_w, r_qlT], w=[bres[bB]], sig=(c == 2))
                P.op("dve", lambda e, h=h, bA=bA: e.tensor_tensor(out=QT[0:64, h, :], in0=banks[bA][0:64, :], in1=CR[0:64, :], op=ALU.mult),
                     r=[bres[bA], r_CR], w=[r_QTh[h]])
                ti = h % 2
                P.op("dve", lambda e, ti=ti, bA=bA: e.tensor_tensor(out=t2[ti][64:96, :], in0=banks[bA][64:96, :], in1=CR[64:96, :],
                                                                   op=ALU.mult), r=[bres[bA], r_CR], w=[r_t2[ti]])
                P.op("dve", lambda e, h=h, bB=bB: e.tensor_tensor(out=QT[64:96, h, :], in0=banks[bB][64:96, :], in1=SR[64:96, :],
                                                                 op=ALU.mult), r=[bres[bB], r_SR], w=[r_QTh[h]])
                P.op("dve", lambda e, h=h, ti=ti: e.tensor_tensor(out=QT[64:96, h, :], in0=QT[64:96, h, :], in1=t2[ti][64:96, :],
                                                                 op=ALU.add), r=[r_t2[ti]], w=[r_QTh[h]])
            if dbg and j == NSLOT - 1:
                out_toks.append(P.dma("sp", lambda e: e.dma_start(out=dbg_out["d_QT"], in_=big16[:]), r=r_QTh))

        def stageEFG(j):
            tiles = [(kt, 0, 0) for kt in range(8 * j)] + [(8 * j + m, SFX[m], 1) for m in range(8)]
            npair = len(tiles) // 2
            for h in range(NH):
                obase = 4 + 2 * (h % 2)
                loaded = {}

                def kvload(c, h=h):
                    ki = kvc["i"] % NKB
                    kvc["i"] += 1
                    P.dma("sp", lambda e, ki=ki, c=c, h=h: e.dma_start(out=kvK[ki][:], in_=KT_d[h, :, c * 1024:(c + 1) * 1024]),
                          w=[r_kv[ki]])
                    P.dma("sp", lambda e, ki=ki, c=c, h=h: e.dma_start(
                        out=kvV[ki][:], in_=VS_d[h, :, c * 8 * 130:(c + 1) * 8 * 130].rearrange("p (t c) -> p t c", c=130)),
                        w=[r_kv[ki]])
                    return ki

                def mm1(p, h=h, loaded=loaded, kvload=kvload):
                    it = itc["i"] + p
                    sp_ = (it % 2) * 2
                    pi = it % NPB
                    s0, ver = tiles[2 * p][1], tiles[2 * p][2]
                    c0 = s0 * 128
                    nr = 128 if ver else 96
                    for t in range(2):
                        kt = tiles[2 * p + t][0]
                        c = kt // 8
                        if c not in loaded:
                            loaded[c] = kvload(c)
                        ki = loaded[c]
                        P.op("pe", lambda e, ki=ki, kt=kt, nr=nr, c0=c0, b=sp_ + t, h=h: e.matmul(
                            banks[b][:, c0:512], lhsT=kvK[ki][0:nr, (kt % 8) * 128:(kt % 8 + 1) * 128],
                            rhs=QT[0:nr, h, c0:512], start=True, stop=True),
                            r=[r_kv[ki], r_QTh[h]], w=([bres[sp_], bres[sp_ + 1]] if t == 0 else []), sig=(t == 1))
                    src = psall[:, sp_ * 512:(sp_ + 2) * 512].rearrange("p (t c) -> p t c", c=512)[:, :, c0:512]
                    P.op("act", lambda e, c0=c0, src=src, pi=pi: e.activation(out=pTs[pi][:, :, c0:512], in_=src,
                                                                             func=AF.Exp, scale=SCALE),
                         r=[bres[sp_], bres[sp_ + 1]], w=[r_pT[pi]])

                def mm2(p, h=h, loaded=loaded, obase=obase):
                    it = itc["i"] + p
                    pi = it % NPB
                    s0 = tiles[2 * p][1]
                    first = True
                    for t in range(2):
                        kt = tiles[2 * p + t][0]
                        ki = loaded[kt // 8]
                        for i in range(s0, 4):
                            ob = obase + i // 2
                            last = (t == 1 and i == 3)
                            P.op("pe", lambda e, ki=ki, kt=kt, i=i, ob=ob, pi=pi, t=t, p=p: e.matmul(
                                banks[ob][:, (i % 2) * 129:(i % 2) * 129 + 129], lhsT=pTs[pi][:, t, i * 128:(i + 1) * 128],
                                rhs=kvV[ki][:, kt % 8, 0:129], start=(p == 0 and t == 0 and i % 2 == 0),
                                stop=(p == npair - 1 and t == 1), skip_group_check=True),
                                r=[r_kv[ki], r_pT[pi]], w=([bres[obase], bres[obase + 1]] if first else []), sig=last)
                            first = False

                mm1(0)
                if npair > 1:
                    mm1(1)
                for p in range(npair):
                    if p + 2 < npair:
                        mm1(p + 2)
                    mm2(p)
                nit = npair
                itc["i"] += nit
                for half in range(2):
                    ob = obase + half
                    ov = banks[ob][:, 0:258].rearrange("p (i c) -> p i c", c=129)
                    P.op("dve", lambda e, ov=ov, half=half: e.reciprocal(out=rl[:, half * 2:half * 2 + 2], in_=ov[:, :, 128]),
                         r=[bres[ob]], w=[r_rl])
                    for i2 in range(2):
                        i = half * 2 + i2
                        P.op("dve", lambda e, ov=ov, i=i, i2=i2, h=h, half=half: e.tensor_scalar(
                            out=otok[:, i, h * 128:(h + 1) * 128], in0=ov[:, i2, 0:128], scalar1=rl[:, half * 2 + i2:half * 2 + i2 + 1],
                            scalar2=None, op0=ALU.mult), r=[bres[ob], r_rl], w=[r_otok])
            if dbg and j == NSLOT - 1:
                out_toks.append(P.dma("sp", lambda e: e.dma_start(out=dbg_out["d_otok"], in_=uT[:].rearrange("p k t -> p (k t)")), r=[r_otok]))
            for i in range(4):
                P.op("act", lambda e, i=i: e.activation(out=junk[:], in_=otok[:, i, :], func=AF.Square, accum_out=r12[:, 8 + i:9 + i]),
                     r=[r_otok], w=[r_junk, r_r1[i]])
                rsqrt_col(r12[:, 8 + i:9 + i], r12[:, i:i + 1], 1, [r_r1[i]], [r_r1[i]], 1.0 / 1024, eng_a="pool")
                b = (6, 7)[i % 2]
                pt = banks[b][:].bitcast(BF16)
                for h in range(NH):
                    P.op("pe", lambda e, pt=pt, h=h, i=i: e.transpose(out=pt[:, h * 128:(h + 1) * 128],
                                                                     in_=otok[:, i, h * 128:(h + 1) * 128], identity=ident[:]),
                         r=[r_otok, r_const], w=[bres[b]], sig=(h == 7))
                P.op("dve", lambda e, pt=pt, i=i: e.tensor_tensor(out=mixT[:, 0:8, i * 128:(i + 1) * 128],
                                                                 in0=pt.rearrange("p (k t) -> p k t", t=128),
                                                                 in1=sgm[:, :, i * 128:(i + 1) * 128], op=ALU.mult),
                     r=[bres[b], r_sgm], w=[r_mixT])
            if dbg and j == NSLOT - 1:
                out_toks.append(P.dma("sp", lambda e: e.dma_start(out=dbg_out["d_mixT"], in_=mixT[:].rearrange("p k t -> p (k t)")), r=[r_mixT]))
        def stageG_items(j):
            return [(lambda i=i: gsub(j, i)) for i in range(4)]

        def gsub(j, i):
            if True:
                c = xcnt["i"]
                xcnt["i"] += 1
                xi = c % 2
                hi = i % 2
                P.dma("sp", lambda e, xi=xi, i=i, j=j: e.dma_start(out=xs[xi][:], in_=xq[(j * 4 + i) * 128:(j * 4 + i + 1) * 128, :]),
                      w=[r_xs[xi]])
                assert len(xring["bufs"]) == 2
                pw = [nb(B6) for _ in range(4)]
                for br in range(2):
                    for half in range(2):
                        b = pw[br * 2 + half]
                        for kk in range(8):
                            P.op("pe", lambda e, br=br, half=half, kk=kk, b=b, i=i: e.matmul(
                                banks[b][:], lhsT=mixT[:, br * 8 + kk, i * 128:(i + 1) * 128],
                                rhs=wout[:, br * 8 + kk, half * 512:(half + 1) * 512], start=(kk == 0), stop=(kk == 7)),
                                r=[r_mixT, r_w], w=[bres[b]], sig=(kk == 7))
                for half in range(2):
                    P.op("dve", lambda e, half=half, b=pw[half], i=i, xi=xi, hi=hi: e.scalar_tensor_tensor(
                        out=hb[hi][:, half * 512:(half + 1) * 512], in0=banks[b][:], scalar=r12[:, i:i + 1],
                        in1=xs[xi][:, half * 512:(half + 1) * 512], op0=ALU.mult, op1=ALU.add),
                        r=[bres[pw[half]], r_r1[i], r_xs[xi]], w=[r_hb[hi]])
                for half in range(2):
                    P.op("dve", lambda e, half=half, b=pw[2 + half], i=i, hi=hi: e.scalar_tensor_tensor(
                        out=hb[hi][:, half * 512:(half + 1) * 512], in0=banks[b][:], scalar=r12[:, 4 + i:5 + i],
                        in1=hb[hi][:, half * 512:(half + 1) * 512], op0=ALU.mult, op1=ALU.add),
                        r=[bres[pw[2 + half]], r_r2], w=[r_hb[hi]])
                P.op("act", lambda e, hi=hi, i=i: e.activation(out=junk[:], in_=hb[hi][:], func=AF.Square, accum_out=r12[:, 12 + i:13 + i]),
                     r=[r_hb[hi]], w=[r_junk, r_rf[i]])
                rsqrt_col(r12[:, 12 + i:13 + i], r12[:, 12 + i:13 + i], 1, [r_rf[i]], [r_rf[i]], 1.0 / 1024, eng_a="pool")
                P.op("dve", lambda e, hi=hi, i=i: e.scalar_tensor_tensor(out=hb[hi][:], in0=hb[hi][:], scalar=r12[:, 12 + i:13 + i],
                                                                          in1=gfin[:], op0=ALU.mult, op1=ALU.mult),
                     r=[r_rf[i], r_const], w=[r_hb[hi]])
                out_toks.append(P.dma("act", lambda e, hi=hi, i=i, j=j: e.dma_start(
                    out=out_d[(j * 4 + i) * 128:(j * 4 + i + 1) * 128, :], in_=hb[hi][:]), r=[r_hb[hi]], w=[r_out]))

        stageA(0)
        pendingB = stageB_items(0)
        for j in range(NSLOT):
            for it_ in pendingB:
                it_()
            stageCD(j)
            if j + 1 < NSLOT:
                stageA(j + 1)
            stageEFG(j)
            g_items = stageG_items(j)
            nb_items = stageB_items(j + 1) if j + 1 < NSLOT else []
            k = 0
            for gi_ in g_items:
                gi_()
                if k < len(nb_items) and k < 3:
                    nb_items[k]()
                    k += 1
            pendingB = nb_items[k:]
        P.wait_all("sp", out_toks)
        P.replay()
    return nc


def _rope_tables(S):
    pos = np.arange(S, dtype=np.float32)
    inv = (np.float32(10000.0) ** (-np.arange(0, 32, 2, dtype=np.float32) / np.float32(32))).astype(np.float32)
    ang = (pos[:, None] * inv[None, :]).astype(np.float32)
    cos = np.cos(ang.astype(np.float64)).astype(np.float32)
    sin = np.sin(ang.astype(np.float64)).astype(np.float32)
    c2 = np.concatenate([cos, cos], 1).T
    s2 = np.concatenate([sin, sin], 1).T
    return np.ascontiguousarray(c2), np.ascontiguousarray(s2)


def make_inputs(NSB, x, norm_in_g, w_in, q_norm_g, w_uq, kv_norm_g, w_ukv, gmlp_ln_g, gmlp_ln_b, w_spatial, b_spatial,
                out_norm_mla_g, out_norm_gmlp_g, w_out, final_norm_g):
    f = np.float32
    S = NSB * 1024
    x = np.asarray(x, f)
    w_in = np.asarray(w_in, f)
    B = x.shape[0]
    c2, s2 = _rope_tables(S)
    pk = lambda w: np.ascontiguousarray(w.reshape(w.shape[0] // 128, 128, w.shape[1]).transpose(1, 0, 2))
    q_lat, kv_lat, k_r = w_in[:, 0:384], w_in[:, 384:640], w_in[:, 640:672]
    g_mla, u_, v_, g_gm = w_in[:, 672:1696], w_in[:, 1696:2720], w_in[:, 2720:3744], w_in[:, 3744:4768]
    krA = np.zeros((1024, 128), f)
    krA[:, 64:96] = k_r
    krB = np.zeros((1024, 128), f)
    krB[:, 64:80] = k_r[:, 16:32]
    krB[:, 80:96] = k_r[:, 0:16]
    w_kvr = pk(np.concatenate([kv_lat, krA, krB], 1))
    groups = [np.concatenate([q_lat, np.zeros((1024, 128), f)], 1), g_mla[:, :512], g_mla[:, 512:], u_[:, :512], u_[:, 512:],
              g_gm[:, :512], g_gm[:, 512:], v_[:, :512], v_[:, 512:]]
    w_main = np.stack([pk(g) for g in groups], 0)
    w_uq = np.asarray(w_uq, f)
    wA = np.zeros((384, NH, 128), f)
    wB = np.zeros((384, NH, 128), f)
    for h in range(NH):
        nope = w_uq[:, h * 96:h * 96 + 64]
        rp = w_uq[:, h * 96 + 64:h * 96 + 96]
        wA[:, h, 0:64] = nope
        wA[:, h, 64:96] = rp
        wB[:, h, 64:80] = rp[:, 16:32]
        wB[:, h, 80:96] = rp[:, 0:16]
    w_ukv = np.asarray(w_ukv, f)
    wk = np.zeros((256, NH, 128), f)
    wv = np.zeros((256, NH, 128), f)
    for h in range(NH):
        wk[:, h, 0:64] = w_ukv[:, h * 192:h * 192 + 64]
        wv[:, h, :] = w_ukv[:, h * 192 + 64:h * 192 + 192]
    col = lambda g: np.ascontiguousarray(np.asarray(g, f).reshape(-1, 128).T)
    kaug = np.zeros((32, 1024), f)
    for c in range(16):
        kaug[c, c * 64:(c + 1) * 64] = 1.0
    shared = dict(
        kaug=kaug, ident=np.eye(128, dtype=f), w_kvr=w_kvr, w_main=w_main, g_in=col(norm_in_g),
        w_uqA=pk(wA.reshape(384, 1024)), w_uqB=pk(wB.reshape(384, 1024)), g_q=col(q_norm_g),
        w_k=pk(wk.reshape(256, 1024)), w_v=pk(wv.reshape(256, 1024)), g_kv=col(kv_norm_g),
        wsT=np.ascontiguousarray(np.asarray(w_spatial, f).transpose(2, 0, 1)),
        ln_b=np.asarray(gmlp_ln_b, f).reshape(1, 1024), bsp=np.asarray(b_spatial, f).reshape(1, 1024),
        lg=col(gmlp_ln_g), w_out=pk(np.asarray(w_out, f)),
        g_out=col(np.concatenate([np.asarray(out_norm_mla_g, f), np.asarray(out_norm_gmlp_g, f)])),
        g_fin=np.asarray(final_norm_g, f).reshape(1, 1024),
        ropek=np.stack([c2, s2], 0),
    )
    in_maps, rows_all = [], []
    for core in range(2 * B):
        b, p = core // 2, core % 2
        rows = np.concatenate([np.arange((8 * j + t) * 128, (8 * j + t + 1) * 128) for j in range(NSB) for t in TILES_P[p]])
        rows_all.append((b, rows))
        qa = np.zeros((32, 512), f)
        for i, t in enumerate(TILES_P[p]):
            for hh in range(2):
                qc = 2 * t + hh
                for c in range(16):
                    if qc < c:
                        qa[c, i * 128 + hh * 64:i * 128 + (hh + 1) * 64] = -BIG
        rq_ = np.stack([c2[:, rows], s2[:, rows]], 1)
        rq_ = np.ascontiguousarray(rq_.reshape(32, 2, NSB, 512).transpose(2, 0, 1, 3))
        m = dict(shared)
        m.update(xk=np.ascontiguousarray(x[b]), xq=np.ascontiguousarray(x[b][rows]), qaug=qa, ropeq=rq_)
        in_maps.append(m)
    return in_maps, rows_all


_NC_CACHE = {}


def kernel(**inputs):
    x = np.asarray(inputs["x"], np.float32)
    B, S, _ = x.shape
    NSB = S // 1024
    if NSB not in _NC_CACHE:
        _NC_CACHE[NSB] = build(NSB)
    nc = _NC_CACHE[NSB]
    in_maps, rows_all = make_inputs(NSB, **inputs)
    res = run_bass_kernel_spmd(nc, in_maps, core_ids=list(range(len(in_maps))))
    out = np.empty((B, S, D), np.float32)
    for core, (b, rows) in enumerate(rows_all):
        out[b, rows] = res.results[core]["out"]
    return out
```

```python
import contextlib
import numpy as np
import concourse.bass as bass
import concourse.mybir as mybir
from concourse.bass_utils import run_bass_kernel_spmd

F32 = mybir.dt.float32
BF16 = mybir.dt.bfloat16
AF = mybir.ActivationFunctionType
ALU = mybir.AluOpType

D = 1024
NH = 8
EPS = 1e-6
SCALE = 1.0 / float(np.sqrt(96.0))
BIG = 30000.0
TILES_P = ((0, 3, 4, 7), (1, 2, 5, 6))
SFX = (0, 0, 1, 1, 2, 2, 3, 3)
N_WG = 9


class Tok:
    __slots__ = ("sem", "val", "eng")

    def __init__(self, sem, val, eng):
        self.sem, self.val, self.eng = sem, val, eng


class Res:
    __slots__ = ("name", "w", "r", "excl")

    def __init__(self, name, excl=False):
        self.name, self.w, self.r, self.excl = name, None, {}, excl


class Prog:
    ENG = ("pe", "act", "dve", "pool", "sp")
    ATTR = {"pe": "tensor", "act": "scalar", "dve": "vector", "pool": "gpsimd", "sp": "sync"}

    def __init__(self, nc, stack):
        self.nc, self.stack = nc, stack
        self.ops = {e: [] for e in self.ENG}
        self.sem = {e: stack.enter_context(nc.semaphore("s_" + e)) for e in self.ENG}
        self.cnt = {e: 0 for e in self.ENG}
        self.pending = {e: Tok(self.sem[e], None, e) for e in self.ENG}
        self.dsems = []
        self.dcount = {}
        self.dnext = 0
        self.ndma = 0

    def add_dma_sems(self, n):
        for i in range(n):
            s = self.stack.enter_context(self.nc.semaphore("dq%d" % len(self.dsems)))
            self.dsems.append(s)
            self.dcount[id(s)] = 0

    def _deps(self, r, w, extra):
        deps = []
        for x in r:
            if x.excl:
                w = list(w) + [x]
                continue
            if x.w is not None:
                deps.append(x.w)
        for x in w:
            if x.w is not None:
                deps.append(x.w)
            deps.extend(x.r.values())
        deps.extend(t for t in extra if t is not None)
        return deps, [x for x in r if not x.excl], w

    def op(self, eng, fn, r=(), w=(), sig=True, extra=()):
        deps, r, w = self._deps(r, w, extra)
        tok = self.pending[eng]
        for x in r:
            x.r[eng] = tok
        for x in w:
            x.w, x.r = tok, {}
        self.ops[eng].append([fn, deps, (self.sem[eng], 1) if sig else None])
        if sig:
            self.cnt[eng] += 1
            tok.val = self.cnt[eng]
            self.pending[eng] = Tok(self.sem[eng], None, eng)
        return tok

    def dma(self, eng, fn, r=(), w=(), extra=()):
        deps, r, w = self._deps(r, w, extra)
        s = self.dsems[self.dnext % len(self.dsems)]
        self.dnext += 1
        prev = self.dcount[id(s)]
        if prev:
            deps.append(Tok(s, prev, "dmaprev"))
        self.dcount[id(s)] = prev + 16
        self.ndma += 1
        tok = Tok(s, prev + 16, "dma%d" % self.ndma)
        for x in r:
            x.r[tok.eng] = tok
        for x in w:
            x.w, x.r = tok, {}
        self.ops[eng].append([fn, deps, (s, 16)])
        return tok

    def barrier(self):
        toks = [Tok(self.sem[e], self.cnt[e], "bar") for e in self.ENG if self.cnt[e] > 0]
        toks += [Tok(s, self.dcount[id(s)], "bar") for s in self.dsems if self.dcount[id(s)] > 0]
        for e in self.ENG:
            assert self.pending[e].val is None
            self.ops[e].append([None, list(toks), None])

    def wait_all(self, eng, toks):
        self.ops[eng].append([None, [t for t in toks if t is not None], None])

    def replay(self):
        nc = self.nc
        for e in self.ENG:
            assert all(True for _ in self.ops[e])
        with nc.Block() as block:
            for e in self.ENG:
                ops = self.ops[e]

                def body(engine, ops=ops, e=e):
                    waited = {}
                    for fn, deps, inc in ops:
                        for t in deps:
                            if t.eng == e and e == "pe":
                                continue
                            assert t.val is not None, "unresolved token"
                            k = id(t.sem)
                            if waited.get(k, -1) >= t.val:
                                continue
                            waited[k] = t.val
                            engine.wait_ge(t.sem, t.val)
                        if fn is not None:
                            ins = fn(engine)
                            if inc is not None:
                                ins.then_inc(inc[0], inc[1])
                getattr(block, self.ATTR[e])(body)


def build(NSB=8, dbg=False):
    S = NSB * 1024
    NBLK = S // 512
    NKT = S // 128
    NSLOT = NSB
    nc = bass.Bass("TRN2", target_bir_lowering=False)

    def din(name, shape, dt=F32):
        return nc.dram_tensor(name, list(shape), dt, kind="ExternalInput").ap()

    xk = din("xk", [S, D])
    xq = din("xq", [NSLOT * 512, D])
    ropek = din("ropek", [2, 32, S])
    ropeq = din("ropeq", [NSLOT, 32, 2, 512])
    kaug_d = din("kaug", [32, 1024])
    qaug_d = din("qaug", [32, 512])
    ident_d = din("ident", [128, 128])
    wkvr_d = din("w_kvr", [128, 8, 512])
    wmain_d = din("w_main", [N_WG, 128, 8, 512])
    gin_d = din("g_in", [128, 8])
    wqa_d = din("w_uqA", [128, 3, 1024])
    wqb_d = din("w_uqB", [128, 3, 1024])
    gq_d = din("g_q", [128, 3])
    wk_d = din("w_k", [128, 2, 1024])
    wv_d = din("w_v", [128, 2, 1024])
    gkv_d = din("g_kv", [128, 2])
    wst_d = din("wsT", [128, 8, 128])
    lb_d = din("ln_b", [1, 1024])
    bsp_d = din("bsp", [1, 1024])
    lg_d = din("lg", [128, 8])
    wout_d = din("w_out", [128, 16, 1024])
    gout_d = din("g_out", [128, 16])
    gfin_d = din("g_fin", [1, 1024])
    out_d = nc.dram_tensor("out", [NSLOT * 512, D], F32, kind="ExternalOutput").ap()

    wbf_d = nc.dram_tensor("wbf", [N_WG, 128, 4096], BF16, kind="Internal").ap()
    KT_d = nc.dram_tensor("KTs", [NH, 128, S], BF16, kind="Internal").ap()
    VS_d = nc.dram_tensor("VSs", [NH, 128, NKT * 130], BF16, kind="Internal").ap()

    dbg_out = {}
    if dbg:
        dbg_out["d_KT"] = nc.dram_tensor("d_KT", [NH, 128, S], BF16, kind="ExternalOutput").ap()
        dbg_out["d_VS"] = nc.dram_tensor("d_VS", [NH, 128, NKT * 130], BF16, kind="ExternalOutput").ap()
        dbg_out["d_QT"] = nc.dram_tensor("d_QT", [128, NH * 2 * 512], BF16, kind="ExternalOutput").ap()
        dbg_out["d_otok"] = nc.dram_tensor("d_otok", [128, 4 * 1024], BF16, kind="ExternalOutput").ap()
        dbg_out["d_mixT"] = nc.dram_tensor("d_mixT", [128, 16 * 512], BF16, kind="ExternalOutput").ap()

    with contextlib.ExitStack() as st:
        P = Prog(nc, st)
        P.add_dma_sems(40)

        def sb(name, shape, dt):
            return st.enter_context(nc.sbuf_tensor("sb_" + name, list(shape), dt))

        def bc_free(t, col, n):
            return bass.AP(t, col, [[t.shape[1], 128], [0, n]])

        psall = st.enter_context(nc.psum_tensor("psall", [128, 4096], F32))
        banks = [psall[:, i * 512:(i + 1) * 512] for i in range(8)]
        bres = [Res("bank%d" % i, excl=True) for i in range(8)]
        rot = {"i": 0}

        def nb(lst=(0, 1, 2, 3, 4, 5, 6)):
            b = lst[rot["i"] % len(lst)]
            rot["i"] += 1
            return b

        ident = sb("ident", [128, 128], BF16)
        ones = sb("ones", [128, 128], BF16)
        nhalf = sb("nhalf", [128, 8], F32)
        epst = sb("epst", [128, 1], F32)
        junk = sb("junk", [128, 1024], BF16)
        wqA = sb("wqA", [128, 3, 1024], BF16)
        wqB = sb("wqB", [128, 3, 1024], BF16)
        wsT = sb("wsTb", [128, 8, 128], BF16)
        B2 = sb("B2", [128, 8, 128], F32)
        lg = sb("lg", [128, 8], F32)
        wout = sb("woutb", [128, 16, 1024], BF16)
        gfin = sb("gfin", [128, 1024], F32)
        kaug = sb("kaugb", [32, 1024], BF16)
        qaug = sb("qaugb", [32, 512], BF16)
        gin = sb("gin", [128, 8], F32)
        gq = sb("gq", [128, 3], F32)
        gkv = sb("gkv", [128, 2], F32)
        gout = sb("gout", [128, 16], F32)
        r_const = Res("const")
        r_w = Res("weights")
        r_junk = Res("junk")
        arena = sb("arena", [128, 8192], BF16)
        r_arena = [Res("arena0"), Res("arena1")]

        def arena_view(off, a, b_):
            return arena[:, off:off + a * b_].rearrange("p (a b) -> p a b", b=b_)

        wkvr = arena_view(0, 8, 512)
        wk = arena_view(4096, 2, 1024)
        wv = arena_view(6144, 2, 1024)
        wring = [arena_view(0, 8, 512), arena_view(4096, 8, 512)]
        xs = [sb("xs%d" % i, [128, 1024], F32) for i in range(2)]
        r_xs = [Res("xs%d" % i) for i in range(2)]
        yb = [sb("yb%d" % i, [128, 1024], BF16) for i in range(2)]
        r_yb = [Res("yb%d" % i) for i in range(2)]
        yT = sb("yT", [128, 8, 512], BF16)
        r_yT = Res("yT")
        ssq = sb("ssq", [128, 8], F32)
        rstd = sb("rstd", [128, 8], F32)
        r_ssq = [Res("ssq%d" % i) for i in range(8)]
        r_rstd = [Res("rstd%d" % i) for i in range(8)]
        r_wbf = Res("wbf")

        with contextlib.ExitStack() as st0:
            NSTG = 4
            stage = [st0.enter_context(nc.sbuf_tensor("stage%d" % i, [128, 4096], F32)) for i in range(NSTG)]
            r_stage = [Res("stage%d" % i) for i in range(NSTG)]
            wtmp = [st0.enter_context(nc.sbuf_tensor("wtmp%d" % i, [128, 4096], BF16)) for i in range(2)]
            r_wtmp = [Res("wtmp%d" % i) for i in range(2)]
            lbb = st0.enter_context(nc.sbuf_tensor("lbb", [128, 1024], BF16))
            r_lbb = Res("lbb")
            P.dma("sp", lambda e: e.dma_start(out=stage[0][:, 0:128], in_=ident_d), w=[r_stage[0]])
            P.op("dve", lambda e: e.tensor_copy(out=ident[:], in_=stage[0][:, 0:128]), r=[r_stage[0]], w=[r_const])
            P.op("pool", lambda e: e.memset(ones[:], 1.0), w=[r_const])
            P.op("pool", lambda e: e.memset(nhalf[:], -0.5), w=[r_const])
            P.op("pool", lambda e: e.memset(epst[:], EPS), w=[r_const])
            for (dst, src) in ((gin, gin_d), (gq, gq_d), (gkv, gkv_d), (gout, gout_d), (lg, lg_d)):
                P.dma("sp", lambda e, dst=dst, src=src: e.dma_start(out=dst[:], in_=src), w=[r_const])
            P.dma("sp", lambda e: e.dma_start(out=gfin[:], in_=bass.AP(gfin_d.tensor, 0, [[0, 128], [1, 1024]])), w=[r_const])
            P.dma("sp", lambda e: e.dma_start(out=stage[1][0:32, 0:1024], in_=kaug_d), w=[r_stage[1]])
            P.dma("sp", lambda e: e.dma_start(out=stage[1][0:32, 1024:1536], in_=qaug_d), w=[r_stage[1]])
            P.op("dve", lambda e: e.tensor_copy(out=kaug[:], in_=stage[1][0:32, 0:1024]), r=[r_stage[1]], w=[r_const])
            P.op("dve", lambda e: e.tensor_copy(out=qaug[:], in_=stage[1][0:32, 1024:1536]), r=[r_stage[1]], w=[r_const])
            sidx = {"i": 0}

            def load_scale_cast(src_ap, ncols, nk, g_tile, g_off, dst_ap, eng="dve", wres=None):
                i = sidx["i"] % NSTG
                sidx["i"] += 1
                stg = stage[i][:, 0:nk * ncols].rearrange("p (k c) -> p k c", c=ncols)
                P.dma("sp", lambda e: e.dma_start(out=stg, in_=src_ap), w=[r_stage[i]])
                gb = bass.AP(g_tile, g_off, [[g_tile.shape[1], 128], [1, nk], [0, ncols]])
                return P.op(eng, lambda e: e.tensor_tensor(out=dst_ap, in0=stg, in1=gb, op=ALU.mult),
                            r=[r_stage[i], r_const], w=[wres if wres is not None else r_w])

            load_scale_cast(wkvr_d, 512, 8, gin, 0, wkvr)
            P.op("dve", lambda e: e.tensor_scalar(out=wkvr[:, :, 448:464], in0=wkvr[:, :, 448:464], scalar1=-1.0,
                                                   scalar2=None, op0=ALU.mult), r=[r_w], w=[r_w])
            load_scale_cast(wk_d, 1024, 2, gkv, 0, wk)
            load_scale_cast(wv_d, 1024, 2, gkv, 0, wv)
            load_scale_cast(wqa_d, 1024, 3, gq, 0, wqA[:])
            load_scale_cast(wqb_d, 1024, 3, gq, 0, wqB[:])
            wqB4 = wqB[:].rearrange("p k (h c) -> p k h c", c=128)
            for kk in range(3):
                P.op("dve", lambda e, kk=kk: e.tensor_scalar(out=wqB4[:, kk, :, 64:80], in0=wqB4[:, kk, :, 64:80],
                                                              scalar1=-1.0, scalar2=None, op0=ALU.mult), r=[r_w], w=[r_w])
            for q in range(4):
                load_scale_cast(wout_d[:, q * 4:(q + 1) * 4, :], 1024, 4, gout, q * 4, wout[:, q * 4:(q + 1) * 4, :])
            def wm_load(gi):
                i = sidx["i"] % NSTG
                sidx["i"] += 1
                stg = stage[i][:, 0:4096].rearrange("p (k c) -> p k c", c=512)
                P.dma("sp", lambda e, stg=stg, gi=gi: e.dma_start(out=stg, in_=wmain_d[gi]), w=[r_stage[i]])
                return i, stg

            wml = {gi: wm_load(gi) for gi in range(min(NSTG - 1, N_WG))}
            for gi in range(N_WG):
                i, stg = wml.pop(gi)
                wt3 = wtmp[gi % 2][:].rearrange("p (k c) -> p k c", c=512)
                gb = bass.AP(gin, 0, [[8, 128], [1, 8], [0, 512]])
                P.op("dve" if gi % 3 != 2 else "pool", lambda e, stg=stg, gb=gb, wt3=wt3: e.tensor_tensor(out=wt3, in0=stg, in1=gb, op=ALU.mult),
                     r=[r_stage[i], r_const], w=[r_wtmp[gi % 2]])
                if gi + NSTG - 1 < N_WG:
                    wml[gi + NSTG - 1] = wm_load(gi + NSTG - 1)
                P.dma("sp", lambda e, gi=gi: e.dma_start(out=wbf_d[gi], in_=wtmp[gi % 2][:]), r=[r_wtmp[gi % 2]], w=[r_wbf])
            i = sidx["i"] % NSTG
            sidx["i"] += 1
            stg = stage[i][:, 0:1024].rearrange("p (g t) -> p g t", t=128)
            P.dma("sp", lambda e: e.dma_start(out=stg, in_=wst_d), w=[r_stage[i]])
            P.op("dve", lambda e: e.tensor_copy(out=wsT[:], in_=stg), r=[r_stage[i]], w=[r_w])
            P.op("dve", lambda e: e.memset(wsT[64:128, :, 0:64], 0.0), w=[r_w])
            i2 = sidx["i"] % NSTG
            sidx["i"] += 1
            lbst = stage[i2]
            P.dma("sp", lambda e: e.dma_start(out=lbst[:, 0:1024], in_=bass.AP(lb_d.tensor, 0, [[0, 128], [1, 1024]])), w=[r_stage[i2]])
            P.dma("sp", lambda e: e.dma_start(out=lbst[:, 1024:2048], in_=bass.AP(bsp_d.tensor, 0, [[0, 128], [1, 1024]])), w=[r_stage[i2]])
            P.op("dve", lambda e: e.tensor_copy(out=lbb[:], in_=lbst[:, 0:1024]), r=[r_stage[i2]], w=[r_lbb])
            for half in range(2):
                b = nb()
                for g4 in range(4):
                    g = half * 4 + g4
                    P.op("pe", lambda e, b=b, g=g, g4=g4: e.matmul(banks[b][:, g4 * 128:(g4 + 1) * 128],
                                                                   lhsT=lbb[:, g * 128:(g + 1) * 128], rhs=wsT[:, g, :],
                                                                   start=True, stop=True),
                         r=[r_lbb, r_w], w=[bres[b]], sig=(g4 == 3))
                P.op("dve", lambda e, b=b, half=half: e.tensor_tensor(
                    out=B2[:, half * 4:(half + 1) * 4, :].rearrange("p g t -> p (g t)"), in0=banks[b][:],
                    in1=lbst[:, 1024 + half * 512:1024 + (half + 1) * 512], op=ALU.add),
                    r=[bres[b], r_stage[i2]], w=[r_const])
            P.barrier()

        xcnt = {"i": 0}
        out_toks = []

        def rsqrt_col(src_ap, dst_ap, n, r_src, r_dst, inv_n, eng_a="dve"):
            P.op(eng_a, lambda e: e.tensor_scalar(out=dst_ap, in0=src_ap, scalar1=inv_n, scalar2=EPS,
                                                  op0=ALU.mult, op1=ALU.add), r=r_src, w=r_dst)
            return P.op("pool", lambda e: e.tensor_tensor(out=dst_ap, in0=dst_ap, in1=nhalf[:, 0:n], op=ALU.pow),
                        r=[r_const], w=r_dst)

        def rsqrt_big(src_ap, dst_ap, r_src, r_dst, inv_n):
            P.op("act", lambda e: e.activation(out=dst_ap, in_=src_ap, func=AF.Ln, scale=inv_n, bias=epst[:, 0:1]),
                 r=r_src + [r_const], w=r_dst)
            return P.op("act", lambda e: e.activation(out=dst_ap, in_=dst_ap, func=AF.Exp, scale=-0.5), w=r_dst)

        xring = {"bufs": list(xs), "res": list(r_xs)}

        def x_loads(src_rows, subs=(0, 1, 2, 3)):
            idx = {}
            for i in subs:
                c = xcnt["i"]
                xcnt["i"] += 1
                xi = c % len(xring["bufs"])
                xb_, xr_ = xring["bufs"][xi], xring["res"][xi]
                P.dma("sp", lambda e, xb_=xb_, i=i: e.dma_start(out=xb_[:], in_=src_rows(i)), w=[xr_])
                idx[i] = (c, xb_, xr_)
            return idx

        def norm_transpose(src_rows, yT=yT, r_yT=r_yT):
            for i in range(4):
                norm_compute(x_loads(src_rows, (i,)), yT, r_yT)

        def norm_compute(idx, yT=yT, r_yT=r_yT):
            for i in sorted(idx):
                c, xb_, xr_ = idx[i]
                xi, si = c % 2, c % 8
                P.op("act", lambda e, xb_=xb_, si=si: e.activation(out=junk[:], in_=xb_[:], func=AF.Square,
                                                                    accum_out=ssq[:, si:si + 1]),
                     r=[xr_], w=[r_junk, r_ssq[si]])
                rsqrt_col(ssq[:, si:si + 1], rstd[:, si:si + 1], 1, [r_ssq[si]], [r_rstd[si]], 1.0 / D, eng_a="pool")
                P.op("dve", lambda e, xi=xi, si=si, xb_=xb_: e.tensor_scalar(out=yb[xi][:], in0=xb_[:], scalar1=rstd[:, si:si + 1],
                                                                              scalar2=None, op0=ALU.mult),
                     r=[xr_, r_rstd[si]], w=[r_yb[xi]])
                b = (6, 7)[c % 2]
                pt = banks[b][:].bitcast(BF16)
                for kk in range(8):
                    P.op("pe", lambda e, pt=pt, kk=kk, xi=xi: e.transpose(out=pt[:, kk * 128:(kk + 1) * 128],
                                                                         in_=yb[xi][:, kk * 128:(kk + 1) * 128], identity=ident[:]),
                         r=[r_yb[xi], r_const], w=[bres[b]], sig=(kk == 7))
                if i % 2 == 0:
                    P.op("dve", lambda e, pt=pt, i=i: e.tensor_copy(out=yT[:, :, i * 128:(i + 1) * 128],
                                                                   in_=pt.rearrange("p (k t) -> p k t", t=128)),
                         r=[bres[b]], w=[r_yT])
                else:
                    P.op("act", lambda e, pt=pt, i=i: e.activation(out=yT[:, :, i * 128:(i + 1) * 128],
                                                                  in_=pt.rearrange("p (k t) -> p k t", t=128), func=AF.Copy),
                         r=[bres[b]], w=[r_yT])

        B6 = (0, 1, 2, 3, 4, 5)
        with contextlib.ExitStack() as st1:
            def sb1(name, shape, dt):
                return st1.enter_context(nc.sbuf_tensor("sb1_" + name, list(shape), dt))

            kvl = sb1("kvl", [128, 2, 512], BF16)
            kvsq = sb1("kvsq", [128, 2, 512], BF16)
            rkv = sb1("rkv", [128, 512], F32)
            rkc = sb1("rkc", [128, 4], F32)
            rtab = [sb1("rtab%d" % i, [128, 2, 512], F32) for i in range(2)]
            krope = [sb1("krope%d" % i, [128, 512], BF16) for i in range(2)]
            r_krope = [Res("krope0"), Res("krope1")]
            rt1 = sb1("rt1", [128, 512], F32)
            rt2 = sb1("rt2", [128, 512], F32)
            kst = [sb1("kst%d" % i, [128, 8, 512], BF16) for i in range(2)]
            vst = [sb1("vst%d" % i, [128, 8, 4, 130], BF16) for i in range(2)]
            r_kvl, r_kvsq, r_rkv, r_rkc = Res("kvl"), Res("kvsq"), Res("rkv"), Res("rkc")
            r_rtab = [Res("rtab0"), Res("rtab1")]
            r_rt1, r_rt2 = Res("rt1"), Res("rt2")
            r_kst = [Res("kst0"), Res("kst1")]
            r_vst = [Res("vst0"), Res("vst1")]
            r_KT, r_VS = Res("KT"), Res("VS")
            for i in range(2):
                P.op("pool", lambda e, i=i: e.memset(vst[i][:], 1.0), w=[r_vst[i]])

            for sbi in range(NSB):
                P.dma("sp", lambda e, sbi=sbi: e.dma_start(
                    out=KT_d.rearrange("h p s -> p h s")[96:128, :, sbi * 1024:(sbi + 1) * 1024],
                    in_=bass.AP(kaug, 0, [[1024, 32], [0, 8], [1, 1024]])), r=[r_const], w=[r_KT])
            yT2 = sb1("yT2", [128, 8, 512], BF16)
            yTs = [(yT, r_yT), (yT2, Res("yT2"))]

            xs1 = [sb1("xs1_%d" % i, [128, 1024], F32) for i in range(2)]
            xring["bufs"] = list(xs) + xs1
            xring["res"] = list(r_xs) + [Res("xs1_0"), Res("xs1_1")]
            xl = {}

            def L1(blk):
                xl[blk] = x_loads(lambda i, blk=blk: xk[(blk * 4 + i) * 128:(blk * 4 + i + 1) * 128, :])

            def C1(blk):
                norm_compute(xl.pop(blk), yT=yTs[blk % 2][0], r_yT=yTs[blk % 2][1])

            L1(0)
            C1(0)
            if NBLK > 1:
                L1(1)
            for blk in range(NBLK):
                pb_ = blk % 2
                if blk + 1 < NBLK:
                    C1(blk + 1)
                if blk + 2 < NBLK:
                    L1(blk + 2)
                yTc, r_yTc = yTs[blk % 2]
                P.dma("sp", lambda e, blk=blk, pb_=pb_: e.dma_start(
                    out=rtab[pb_][64:96, :, :], in_=ropek.rearrange("t r s -> r t s")[:, :, blk * 512:(blk + 1) * 512]),
                    w=[r_rtab[pb_]])
                bk = [nb(B6) for _ in range(4)]
                for c4 in range(4):
                    for kk in range(8):
                        P.op("pe", lambda e, c4=c4, kk=kk, b=bk[c4], yTc=yTc: e.matmul(banks[b][:], lhsT=wkvr[:, kk, c4 * 128:(c4 + 1) * 128],
                                                                             rhs=yTc[:, kk, :], start=(kk == 0), stop=(kk == 7)),
                             r=[r_w, r_yTc], w=[bres[bk[c4]]], sig=(kk == 7))
                for c in range(2):
                    P.op("act", lambda e, c=c, b=bk[c]: e.activation(out=kvl[:, c, :], in_=banks[b][:], func=AF.Copy),
                         r=[bres[bk[c]]], w=[r_kvl])
                    P.op("act", lambda e, c=c, b=bk[c]: e.activation(out=kvsq[:, c, :], in_=banks[b][:], func=AF.Square),
                         r=[bres[bk[c]]], w=[r_kvsq])
                P.op("dve", lambda e, b=bk[2], pb_=pb_: e.tensor_tensor(out=rt1[64:96, :], in0=banks[b][64:96, :],
                                                                       in1=rtab[pb_][64:96, 0, :], op=ALU.mult),
                     r=[bres[bk[2]], r_rtab[pb_]], w=[r_rt1])
                P.op("dve", lambda e, b=bk[3], pb_=pb_: e.tensor_tensor(out=rt2[64:96, :], in0=banks[b][64:96, :],
                                                                       in1=rtab[pb_][64:96, 1, :], op=ALU.mult),
                     r=[bres[bk[3]], r_rtab[pb_]], w=[r_rt2])
                P.op("dve", lambda e, pb_=pb_: e.tensor_tensor(out=krope[pb_][64:96, :], in0=rt1[64:96, :], in1=rt2[64:96, :],
                                                               op=ALU.add), r=[r_rt1, r_rt2], w=[r_krope[pb_]])
                P.dma("sp", lambda e, pb_=pb_, blk=blk: e.dma_start(
                    out=KT_d.rearrange("h p s -> p h s")[64:96, :, blk * 512:(blk + 1) * 512],
                    in_=bass.AP(krope[pb_], 64 * 512, [[512, 32], [0, 8], [1, 512]])), r=[r_krope[pb_]], w=[r_KT])
                b1 = nb(B6)
                for c in range(2):
                    P.op("pe", lambda e, c=c, b1=b1: e.matmul(banks[b1][:], lhsT=ones[:], rhs=kvsq[:, c, :],
                                                             start=(c == 0), stop=(c == 1)),
                         r=[r_kvsq, r_const], w=[bres[b1]], sig=(c == 1))
                rsqrt_big(banks[b1][:], rkv[:], [bres[b1]], [r_rkv], 1.0 / 256)
                b2 = nb(B6)
                for i in range(4):
                    for c in range(2):
                        P.op("pe", lambda e, c=c, i=i, b2=b2: e.matmul(banks[b2][:, i:i + 1], lhsT=kvsq[:, c, i * 128:(i + 1) * 128],
                                                                      rhs=ones[:, 0:1], start=(c == 0), stop=(c == 1)),
                             r=[r_kvsq, r_const], w=[bres[b2]], sig=(i == 3 and c == 1))
                rsqrt_col(banks[b2][:, 0:4], rkc[:], 4, [bres[b2]], [r_rkc], 1.0 / 256)
                for h in range(NH):
                    b = nb(B6)
                    for c in range(2):
                        P.op("pe", lambda e, c=c, h=h, b=b: e.matmul(banks[b][:], lhsT=wk[:, c, h * 128:(h + 1) * 128],
                                                                    rhs=kvl[:, c, :], start=(c == 0), stop=(c == 1)),
                             r=[r_w, r_kvl], w=[bres[b]], sig=(c == 1))
                    P.op("dve", lambda e, h=h, b=b, pb_=pb_: e.tensor_tensor(out=kst[pb_][0:64, h, :], in0=banks[b][0:64, :],
                                                                            in1=rkv[0:64, :], op=ALU.mult),
                         r=[bres[b], r_rkv], w=[r_kst[pb_]])
                P.dma("sp", lambda e, pb_=pb_, blk=blk: e.dma_start(
                    out=KT_d.rearrange("h p s -> p h s")[0:64, :, blk * 512:(blk + 1) * 512], in_=kst[pb_][0:64, :, :]),
                    r=[r_kst[pb_]], w=[r_KT])
                for i in range(4):
                    for half in range(2):
                        b = nb(B6)
                        for c in range(2):
                            P.op("pe", lambda e, c=c, i=i, half=half, b=b: e.matmul(
                                banks[b][:], lhsT=kvl[:, c, i * 128:(i + 1) * 128], rhs=wv[:, c, half * 512:(half + 1) * 512],
                                start=(c == 0), stop=(c == 1)), r=[r_kvl, r_w], w=[bres[b]], sig=(c == 1))
                        if half == 0:
                            P.op("dve", lambda e, i=i, half=half, b=b, pb_=pb_: e.tensor_scalar(
                                out=vst[pb_][:, half * 4:(half + 1) * 4, i, 0:128],
                                in0=banks[b][:].rearrange("p (h c) -> p h c", c=128), scalar1=rkc[:, i:i + 1], scalar2=None,
                                op0=ALU.mult), r=[bres[b], r_rkc], w=[r_vst[pb_]])
                        else:
                            P.op("act", lambda e, i=i, half=half, b=b, pb_=pb_: e.activation(
                                out=vst[pb_][:, half * 4:(half + 1) * 4, i, 0:128],
                                in_=banks[b][:].rearrange("p (h c) -> p h c", c=128), func=AF.Identity, scale=rkc[:, i:i + 1]),
                                r=[bres[b], r_rkc], w=[r_vst[pb_]])
                P.dma("sp", lambda e, pb_=pb_, blk=blk: e.dma_start(
                    out=VS_d.rearrange("h p (t c) -> p h t c", c=130)[:, :, blk * 4:(blk + 1) * 4, :], in_=vst[pb_][:]),
                    r=[r_vst[pb_]], w=[r_VS])
            P.barrier()
            xring["bufs"] = list(xs)
            xring["res"] = list(r_xs)
            xcnt["i"] = 0
            if dbg:
                out_toks.append(P.dma("sp", lambda e: e.dma_start(out=dbg_out["d_KT"], in_=KT_d)))
                out_toks.append(P.dma("sp", lambda e: e.dma_start(out=dbg_out["d_VS"], in_=VS_d)))

        qlT = sb("qlT", [128, 3, 512], BF16)
        sqq = sb("sqq", [128, 3, 512], BF16)
        sgm = sb("sgm", [128, 8, 512], BF16)
        uT = sb("uT", [128, 8, 512], BF16)
        otok = uT[:].rearrange("p k t -> p (k t)").rearrange("p (i f) -> p i f", f=1024)
        sgg = sb("sgg", [128, 8, 512], BF16)
        sggf = sgg[:].rearrange("p k t -> p (k t)")
        hb = [sggf[:, i * 2048:(i + 1) * 2048].bitcast(F32) for i in range(2)]
        big16 = sb("big16", [128, 8192], BF16)
        QT = sb("QT", [128, 8, 512], BF16)
        r_big = [Res("big16")]
        vf = [big16[:, i * 2048:(i + 1) * 2048].bitcast(F32) for i in range(2)]
        vhat = big16[:, 4096:8192].rearrange("p (i f) -> p i f", f=1024)
        mixT = sb("mixT", [128, 16, 512], BF16)
        tmpg = [sb("tmpg%d" % i, [128, 512], F32) for i in range(2)]
        og = [sb("og%d" % i, [128, 512], BF16) for i in range(2)]
        sqg = yT
        rtq = sb("rtq", [128, 2, 512], F32)
        CR = sb("CR", [128, 512], F32)
        SR = sb("SR", [128, 512], F32)
        t2 = [sb("t2_%d" % i, [128, 512], F32) for i in range(2)]
        NKB = 3
        kvK = [sb("kvK%d" % i, [128, 1024], BF16) for i in range(NKB)]
        kvV = [sb("kvV%d" % i, [128, 8, 130], BF16) for i in range(NKB)]
        NPB = 3
        pTs = [sb("pTs%d" % i, [128, 2, 512], BF16) for i in range(NPB)]
        rl = sb("rl", [128, 8], F32)
        bst = sb("bst", [128, 12], F32)
        mv = sb("mv", [128, 4], F32)
        r12 = sb("r12", [128, 16], F32)
        (r_qlT, r_sqq, r_rq, r_sgm, r_uT, r_sgg, r_mixT, r_rtq, r_CR, r_SR, r_rl, r_bst,
         r_mv) = [Res(n) for n in ("qlT", "sqq", "rq", "sgm", "uT", "sgg", "mixT", "rtq", "CR", "SR", "rl", "bst", "mv")]
        r_otok = r_uT
        r_hb = [r_sgg, r_sgg]
        r_sqg = r_yT
        r_tmpg = [Res("tmpg0"), Res("tmpg1")]
        r_og = [Res("og0"), Res("og1")]
        r_t2 = [Res("t2a"), Res("t2b")]
        r_kv = [Res("kv%d" % i) for i in range(NKB)]
        r_pT = [Res("pT%d" % i) for i in range(NPB)]
        r_r1 = [Res("r1_%d" % i) for i in range(4)]
        r_r2 = Res("r2")
        r_rf = [Res("rf%d" % i) for i in range(4)]
        r_QTh = [Res("QT%d" % h) for h in range(NH)]
        r_out = Res("out")
        P.dma("sp", lambda e: e.dma_start(out=QT[96:128, :, :], in_=bass.AP(qaug, 0, [[512, 32], [0, 8], [1, 512]])),
              r=[r_const], w=r_QTh)
        wcnt = {"i": 0}
        itc = {"i": 0}
        kvc = {"i": 0}
        WG_ORDER = (0, 1, 2, 5, 6, 3, 4, 7, 8)

        def wload(gi):
            ri = wcnt["i"] % 2
            wcnt["i"] += 1
            P.dma("sp", lambda e, ri=ri, gi=gi: e.dma_start(out=wring[ri], in_=wbf_d[gi].rearrange("p (k c) -> p k c", c=512)),
                  w=[r_arena[ri]])
            return ri

        def stageA(j):
            norm_transpose(lambda i, j=j: xq[(j * 4 + i) * 128:(j * 4 + i + 1) * 128, :])

        def stageB_items(j):
            items = []

            def grp(gi):
                ri = wload(gi)
                nct = 3 if gi == 0 else 4
                for ct in range(nct):
                    b = nb()
                    for kk in range(8):
                        P.op("pe", lambda e, ri=ri, ct=ct, kk=kk, b=b: e.matmul(banks[b][:], lhsT=wring[ri][:, kk, ct * 128:(ct + 1) * 128],
                                                                               rhs=yT[:, kk, :], start=(kk == 0), stop=(kk == 7)),
                             r=[r_arena[ri], r_yT], w=[bres[b]], sig=(kk == 7))
                    if gi == 0:
                        P.op("act", lambda e, ct=ct, b=b: e.activation(out=qlT[:, ct, :], in_=banks[b][:], func=AF.Copy),
                             r=[bres[b]], w=[r_qlT])
                        P.op("act", lambda e, ct=ct, b=b: e.activation(out=sqq[:, ct, :], in_=banks[b][:], func=AF.Square),
                             r=[bres[b]], w=[r_sqq])
                    else:
                        dst, rr, fn = {1: (sgm, r_sgm, AF.Silu), 2: (sgm, r_sgm, AF.Silu), 5: (sgg, r_sgg, AF.Silu),
                                       6: (sgg, r_sgg, AF.Silu), 3: (uT, r_uT, AF.Gelu), 4: (uT, r_uT, AF.Gelu)}[gi]
                        t = ((gi - 1) % 2) * 4 + ct
                        P.op("act", lambda e, dst=dst, t=t, b=b, fn=fn: e.activation(out=dst[:, t, :], in_=banks[b][:], func=fn),
                             r=[bres[b]], w=[rr])

            for gi in WG_ORDER[:7]:
                items.append(lambda gi=gi: grp(gi))
            rv = []

            def vpart(i):
                if i == 0:
                    rv.extend([wload(7), wload(8)])
                vi = i % 2
                for half in range(2):
                    b = nb()
                    for kk in range(8):
                        P.op("pe", lambda e, i=i, half=half, kk=kk, b=b: e.matmul(
                            banks[b][:], lhsT=yT[:, kk, i * 128:(i + 1) * 128], rhs=wring[rv[half]][:, kk, :],
                            start=(kk == 0), stop=(kk == 7)), r=[r_arena[rv[half]], r_yT], w=[bres[b]], sig=(kk == 7))
                    P.op("act", lambda e, vi=vi, half=half, b=b: e.activation(out=vf[vi][:, half * 512:(half + 1) * 512],
                                                                               in_=banks[b][:], func=AF.Gelu),
                         r=[bres[b]], w=r_big)
                for half in range(2):
                    P.op("dve", lambda e, vi=vi, half=half: e.bn_stats(out=bst[:, half * 6:(half + 1) * 6],
                                                                       in_=vf[vi][:, half * 512:(half + 1) * 512]),
                         r=r_big, w=[r_bst])
                P.op("dve", lambda e: e.bn_aggr(out=mv[:, 0:2], in_=bst[:, 0:12]), r=[r_bst], w=[r_mv])
                rsqrt_col(mv[:, 1:2], mv[:, 2:3], 1, [r_mv], [r_mv], 1.0)
                P.op("pool", lambda e: e.tensor_scalar(out=mv[:, 3:4], in0=mv[:, 0:1], scalar1=mv[:, 2:3], scalar2=-1.0,
                                                        op0=ALU.mult, op1=ALU.mult), r=[r_mv], w=[r_mv])
                P.op("act", lambda e, vi=vi, i=i: e.activation(out=vhat[:, i, :], in_=vf[vi][:], func=AF.Identity,
                                                                scale=mv[:, 2:3], bias=mv[:, 3:4]),
                     r=[r_mv], w=r_big)

            for i in range(4):
                items.append(lambda i=i: vpart(i))
            return items

        def stageC(j):
            for g in range(8):
                b = nb()
                gi2 = g % 2
                for i in range(4):
                    P.op("pe", lambda e, g=g, i=i, b=b: e.matmul(banks[b][:, i * 128:(i + 1) * 128],
                                                                lhsT=vhat[:, i, g * 128:(g + 1) * 128], rhs=wsT[:, g, :],
                                                                start=True, stop=True),
                         r=r_big + [r_w], w=[bres[b]], sig=(i == 3))
                P.op("dve", lambda e, g=g, b=b, gi2=gi2: e.scalar_tensor_tensor(
                    out=tmpg[gi2][:], in0=banks[b][:], scalar=lg[:, g:g + 1],
                    in1=bass.AP(B2, g * 128, [[1024, 128], [0, 4], [1, 128]]), op0=ALU.mult, op1=ALU.add),
                    r=[bres[b], r_const], w=[r_tmpg[gi2]])
                P.op("dve", lambda e, g=g, gi2=gi2: e.tensor_tensor(out=og[gi2][:], in0=tmpg[gi2][:], in1=uT[:, g, :], op=ALU.mult),
                     r=[r_tmpg[gi2], r_uT], w=[r_og[gi2]])
                P.op("pool", lambda e, g=g, gi2=gi2: e.tensor_tensor(out=mixT[:, 8 + g, :], in0=og[gi2][:], in1=sgg[:, g, :], op=ALU.mult),
                     r=[r_og[gi2], r_sgg], w=[r_mixT])
                P.op("act", lambda e, g=g, gi2=gi2: e.activation(out=sqg[:, g, :], in_=og[gi2][:], func=AF.Square),
                     r=[r_og[gi2]], w=[r_sqg])
            b = nb()
            for i in range(4):
                for g in range(8):
                    P.op("pe", lambda e, g=g, i=i, b=b: e.matmul(banks[b][:, i:i + 1], lhsT=sqg[:, g, i * 128:(i + 1) * 128],
                                                                rhs=ones[:, 0:1], start=(g == 0), stop=(g == 7)),
                         r=[r_sqg, r_const], w=[bres[b]], sig=(i == 3 and g == 7))
            rsqrt_col(banks[b][:, 0:4], r12[:, 4:8], 4, [bres[b]], [r_r2], 1.0 / 1024)
        def stageD_items(j):
            return [lambda: d_pre(j)] + [(lambda h=h: d_head(j, h)) for h in range(NH)]

        def d_pre(j):
            P.dma("sp", lambda e, j=j: e.dma_start(out=rtq[64:96, :, :], in_=ropeq[j]), w=[r_rtq])
            b = nb()
            for c in range(3):
                P.op("pe", lambda e, c=c, b=b: e.matmul(banks[b][:], lhsT=ones[:], rhs=sqq[:, c, :], start=(c == 0), stop=(c == 2)),
                     r=[r_sqq, r_const], w=[bres[b]], sig=(c == 2))
            rsqrt_big(banks[b][:], CR[:], [bres[b]], [r_CR], 1.0 / 384)
            P.op("dve", lambda e: e.tensor_tensor(out=SR[64:96, :], in0=rtq[64:96, 1, :], in1=CR[64:96, :], op=ALU.mult),
                 r=[r_CR, r_rtq], w=[r_SR])
            P.op("dve", lambda e: e.tensor_tensor(out=CR[64:96, :], in0=rtq[64:96, 0, :], in1=CR[64:96, :], op=ALU.mult),
                 r=[r_rtq, r_SR], w=[r_CR])

        def d_head(j, h):
            if True:
                bA, bB = nb(), nb()
                for c in range(3):
                    P.op("pe", lambda e, c=c, h=h, bA=bA: e.matmul(banks[bA][:], lhsT=wqA[:, c, h * 128:(h + 1) * 128], rhs=qlT[:, c, :],
                                                                  start=(c == 0), stop=(c == 2)),
                         r=[r_w, r_qlT], w=[bres[bA]], sig=(c == 2))
                for c in range(3):
                    P.op("pe", lambda e, c=c, h=h, bB=bB: e.matmul(banks[bB][:], lhsT=wqB[:, c, h * 128:(h + 1) * 128], rhs=qlT[:, c, :],
                                                                  start=(c == 0), stop=(c == 2)),
                         r=[r_w, r_qlT], w=[bres[bB]], sig=(c == 2))
                P.op("dve", lambda e, h=h, bA=bA: e.tensor_tensor(out=QT[0:64, h, :], in0=banks[bA][0:64, :], in1=CR[0:64, :], op=ALU.mult),
                     r=[bres[bA], r_CR], w=[r_QTh[h]])
                ti = h % 2
                P.op("dve", lambda e, ti=ti, bA=bA: e.tensor_tensor(out=t2[ti][64:96, :], in0=banks[bA][64:96, :], in1=CR[64:96, :],
                                                                   op=ALU.mult), r=[bres[bA], r_CR], w=[r_t2[ti]])
                P.op("dve", lambda e, h=h, bB=bB: e.tensor_tensor(out=QT[64:96, h, :], in0=banks[bB][64:96, :], in1=SR[64:96, :],
                                                                 op=ALU.mult), r=[bres[bB], r_SR], w=[r_QTh[h]])
                P.op("dve", lambda e, h=h, ti=ti: e.tensor_tensor(out=QT[64:96, h, :], in0=QT[64:96, h, :], in1=t2[ti][64:96, :],
                                                                 op=ALU.add), r=[r_t2[ti]], w=[r_QTh[h]])

        def stageEFG(j):
            tiles = [(kt, 0, 0) for kt in range(8 * j)] + [(8 * j + m, SFX[m], 1) for m in range(8)]
            npair = len(tiles) // 2
            for h in range(NH):
                obase = 4 + 2 * (h % 2)
                loaded = {}

                def kvload(c, h=h):
                    ki = kvc["i"] % NKB
                    kvc["i"] += 1
                    P.dma("sp", lambda e, ki=ki, c=c, h=h: e.dma_start(out=kvK[ki][:], in_=KT_d[h, :, c * 1024:(c + 1) * 1024]),
                          w=[r_kv[ki]])
                    P.dma("sp", lambda e, ki=ki, c=c, h=h: e.dma_start(
                        out=kvV[ki][:], in_=VS_d[h, :, c * 8 * 130:(c + 1) * 8 * 130].rearrange("p (t c) -> p t c", c=130)),
                        w=[r_kv[ki]])
                    return ki

                def mm1(p, h=h, loaded=loaded, kvload=kvload):
                    it = itc["i"] + p
                    sp_ = (it % 2) * 2
                    pi = it % NPB
                    s0, ver = tiles[2 * p][1], tiles[2 * p][2]
                    c0 = s0 * 128
                    nr = 128 if ver else 96
                    for t in range(2):
                        kt = tiles[2 * p + t][0]
                        c = kt // 8
                        if c not in loaded:
                            loaded[c] = kvload(c)
                        ki = loaded[c]
                        P.op("pe", lambda e, ki=ki, kt=kt, nr=nr, c0=c0, b=sp_ + t, h=h: e.matmul(
                            banks[b][:, c0:512], lhsT=kvK[ki][0:nr, (kt % 8) * 128:(kt % 8 + 1) * 128],
                            rhs=QT[0:nr, h, c0:512], start=True, stop=True),
                            r=[r_kv[ki], r_QTh[h]], w=([bres[sp_], bres[sp_ + 1]] if t == 0 else []), sig=(t == 1))
                    src = psall[:, sp_ * 512:(sp_ + 2) * 512].rearrange("p (t c) -> p t c", c=512)[:, :, c0:512]
                    P.op("act", lambda e, c0=c0, src=src, pi=pi: e.activation(out=pTs[pi][:, :, c0:512], in_=src,
                                                                             func=AF.Exp, scale=SCALE),
                         r=[bres[sp_], bres[sp_ + 1]], w=[r_pT[pi]])

                def mm2(p, h=h, loaded=loaded, obase=obase):
                    it = itc["i"] + p
                    pi = it % NPB
                    s0 = tiles[2 * p][1]
                    first = True
                    for t in range(2):
                        kt = tiles[2 * p + t][0]
                        ki = loaded[kt // 8]
                        for i in range(s0, 4):
                            ob = obase + i // 2
                            last = (t == 1 and i == 3)
                            P.op("pe", lambda e, ki=ki, kt=kt, i=i, ob=ob, pi=pi, t=t, p=p: e.matmul(
                                banks[ob][:, (i % 2) * 129:(i % 2) * 129 + 129], lhsT=pTs[pi][:, t, i * 128:(i + 1) * 128],
                                rhs=kvV[ki][:, kt % 8, 0:129], start=(p == 0 and t == 0 and i % 2 == 0),
                                stop=(p == npair - 1 and t == 1), skip_group_check=True),
                                r=[r_kv[ki], r_pT[pi]], w=([bres[obase], bres[obase + 1]] if first else []), sig=last)
                            first = False

                mm1(0)
                if npair > 1:
                    mm1(1)
                for p in range(npair):
                    if p + 2 < npair:
                        mm1(p + 2)
                    mm2(p)
                nit = npair
                itc["i"] += nit
                for half in range(2):
                    ob = obase + half
                    ov = banks[ob][:, 0:258].rearrange("p (i c) -> p i c", c=129)
                    P.op("dve", lambda e, ov=ov, half=half: e.reciprocal(out=rl[:, half * 2:half * 2 + 2], in_=ov[:, :, 128]),
                         r=[bres[ob]], w=[r_rl])
                    for i2 in range(2):
                        i = half * 2 + i2
                        P.op("dve", lambda e, ov=ov, i=i, i2=i2, h=h, half=half: e.tensor_scalar(
                            out=otok[:, i, h * 128:(h + 1) * 128], in0=ov[:, i2, 0:128], scalar1=rl[:, half * 2 + i2:half * 2 + i2 + 1],
                            scalar2=None, op0=ALU.mult), r=[bres[ob], r_rl], w=[r_otok])
            if dbg and j == NSLOT - 1:
                out_toks.append(P.dma("sp", lambda e: e.dma_start(out=dbg_out["d_otok"], in_=uT[:].rearrange("p k t -> p (k t)")), r=[r_otok]))
            for i in range(4):
                P.op("act", lambda e, i=i: e.activation(out=junk[:], in_=otok[:, i, :], func=AF.Square, accum_out=r12[:, 8 + i:9 + i]),
                     r=[r_otok], w=[r_junk, r_r1[i]])
                rsqrt_col(r12[:, 8 + i:9 + i], r12[:, i:i + 1], 1, [r_r1[i]], [r_r1[i]], 1.0 / 1024, eng_a="pool")
                b = (6, 7)[i % 2]
                pt = banks[b][:].bitcast(BF16)
                for h in range(NH):
                    P.op("pe", lambda e, pt=pt, h=h, i=i: e.transpose(out=pt[:, h * 128:(h + 1) * 128],
                                                                     in_=otok[:, i, h * 128:(h + 1) * 128], identity=ident[:]),
                         r=[r_otok, r_const], w=[bres[b]], sig=(h == 7))
                P.op("dve", lambda e, pt=pt, i=i: e.tensor_tensor(out=mixT[:, 0:8, i * 128:(i + 1) * 128],
                                                                 in0=pt.rearrange("p (k t) -> p k t", t=128),
                                                                 in1=sgm[:, :, i * 128:(i + 1) * 128], op=ALU.mult),
                     r=[bres[b], r_sgm], w=[r_mixT])
            if dbg and j == NSLOT - 1:
                out_toks.append(P.dma("sp", lambda e: e.dma_start(out=dbg_out["d_mixT"], in_=mixT[:].rearrange("p k t -> p (k t)")), r=[r_mixT]))
        def stageG_items(j):
            return [(lambda i=i: gsub(j, i)) for i in range(4)]

        def gsub(j, i):
            if True:
                c = xcnt["i"]
                xcnt["i"] += 1
                xi = c % 2
                hi = i % 2
                P.dma("sp", lambda e, xi=xi, i=i, j=j: e.dma_start(out=xs[xi][:], in_=xq[(j * 4 + i) * 128:(j * 4 + i + 1) * 128, :]),
                      w=[r_xs[xi]])
                assert len(xring["bufs"]) == 2
                pw = [nb(B6) for _ in range(4)]
                for br in range(2):
                    for half in range(2):
                        b = pw[br * 2 + half]
                        for kk in range(8):
                            P.op("pe", lambda e, br=br, half=half, kk=kk, b=b, i=i: e.matmul(
                                banks[b][:], lhsT=mixT[:, br * 8 + kk, i * 128:(i + 1) * 128],
                                rhs=wout[:, br * 8 + kk, half * 512:(half + 1) * 512], start=(kk == 0), stop=(kk == 7)),
                                r=[r_mixT, r_w], w=[bres[b]], sig=(kk == 7))
                for half in range(2):
                    P.op("dve", lambda e, half=half, b=pw[half], i=i, xi=xi, hi=hi: e.scalar_tensor_tensor(
                        out=hb[hi][:, half * 512:(half + 1) * 512], in0=banks[b][:], scalar=r12[:, i:i + 1],
                        in1=xs[xi][:, half * 512:(half + 1) * 512], op0=ALU.mult, op1=ALU.add),
                        r=[bres[pw[half]], r_r1[i], r_xs[xi]], w=[r_hb[hi]])
                for half in range(2):
                    P.op("dve", lambda e, half=half, b=pw[2 + half], i=i, hi=hi: e.scalar_tensor_tensor(
                        out=hb[hi][:, half * 512:(half + 1) * 512], in0=banks[b][:], scalar=r12[:, 4 + i:5 + i],
                        in1=hb[hi][:, half * 512:(half + 1) * 512], op0=ALU.mult, op1=ALU.add),
                        r=[bres[pw[2 + half]], r_r2], w=[r_hb[hi]])
                P.op("act", lambda e, hi=hi, i=i: e.activation(out=junk[:], in_=hb[hi][:], func=AF.Square, accum_out=r12[:, 12 + i:13 + i]),
                     r=[r_hb[hi]], w=[r_junk, r_rf[i]])
                rsqrt_col(r12[:, 12 + i:13 + i], r12[:, 12 + i:13 + i], 1, [r_rf[i]], [r_rf[i]], 1.0 / 1024, eng_a="pool")
                P.op("dve", lambda e, hi=hi, i=i: e.scalar_tensor_tensor(out=hb[hi][:], in0=hb[hi][:], scalar=r12[:, 12 + i:13 + i],
                                                                          in1=gfin[:], op0=ALU.mult, op1=ALU.mult),
                     r=[r_rf[i], r_const], w=[r_hb[hi]])
                out_toks.append(P.dma("act", lambda e, hi=hi, i=i, j=j: e.dma_start(
                    out=out_d[(j * 4 + i) * 128:(j * 4 + i + 1) * 128, :], in_=hb[hi][:]), r=[r_hb[hi]], w=[r_out]))

        stageA(0)
        pendingB = stageB_items(0)
        for j in range(NSLOT):
            ditems = stageD_items(j)
            if j == 0:
                pendingB.pop(0)()
            while pendingB or ditems:
                if ditems:
                    ditems.pop(0)()
                if pendingB:
                    pendingB.pop(0)()
            stageC(j)
            if j + 1 < NSLOT:
                stageA(j + 1)
            stageEFG(j)
            g_items = stageG_items(j)
            nb_items = stageB_items(j + 1) if j + 1 < NSLOT else []
            k = 0
            for gi_ in g_items:
                gi_()
                if k < len(nb_items) and k < 3:
                    nb_items[k]()
                    k += 1
            pendingB = nb_items[k:]
        P.wait_all("sp", out_toks)
        P.replay()
    return nc


def _rope_tables(S):
    pos = np.arange(S, dtype=np.float32)
    inv = (np.float32(10000.0) ** (-np.arange(0, 32, 2, dtype=np.float32) / np.float32(32))).astype(np.float32)
    ang = (pos[:, None] * inv[None, :]).astype(np.float32)
    cos = np.cos(ang.astype(np.float64)).astype(np.float32)
    sin = np.sin(ang.astype(np.float64)).astype(np.float32)
    c2 = np.concatenate([cos, cos], 1).T
    s2 = np.concatenate([sin, sin], 1).T
    return np.ascontiguousarray(c2), np.ascontiguousarray(s2)


def make_inputs(NSB, x, norm_in_g, w_in, q_norm_g, w_uq, kv_norm_g, w_ukv, gmlp_ln_g, gmlp_ln_b, w_spatial, b_spatial,
                out_norm_mla_g, out_norm_gmlp_g, w_out, final_norm_g):
    f = np.float32
    S = NSB * 1024
    x = np.asarray(x, f)
    w_in = np.asarray(w_in, f)
    B = x.shape[0]
    c2, s2 = _rope_tables(S)
    pk = lambda w: np.ascontiguousarray(w.reshape(w.shape[0] // 128, 128, w.shape[1]).transpose(1, 0, 2))
    q_lat, kv_lat, k_r = w_in[:, 0:384], w_in[:, 384:640], w_in[:, 640:672]
    g_mla, u_, v_, g_gm = w_in[:, 672:1696], w_in[:, 1696:2720], w_in[:, 2720:3744], w_in[:, 3744:4768]
    krA = np.zeros((1024, 128), f)
    krA[:, 64:96] = k_r
    krB = np.zeros((1024, 128), f)
    krB[:, 64:80] = k_r[:, 16:32]
    krB[:, 80:96] = k_r[:, 0:16]
    w_kvr = pk(np.concatenate([kv_lat, krA, krB], 1))
    groups = [np.concatenate([q_lat, np.zeros((1024, 128), f)], 1), g_mla[:, :512], g_mla[:, 512:], u_[:, :512], u_[:, 512:],
              g_gm[:, :512], g_gm[:, 512:], v_[:, :512], v_[:, 512:]]
    w_main = np.stack([pk(g) for g in groups], 0)
    w_uq = np.asarray(w_uq, f)
    wA = np.zeros((384, NH, 128), f)
    wB = np.zeros((384, NH, 128), f)
    for h in range(NH):
        nope = w_uq[:, h * 96:h * 96 + 64]
        rp = w_uq[:, h * 96 + 64:h * 96 + 96]
        wA[:, h, 0:64] = nope
        wA[:, h, 64:96] = rp
        wB[:, h, 64:80] = rp[:, 16:32]
        wB[:, h, 80:96] = rp[:, 0:16]
    w_ukv = np.asarray(w_ukv, f)
    wk = np.zeros((256, NH, 128), f)
    wv = np.zeros((256, NH, 128), f)
    for h in range(NH):
        wk[:, h, 0:64] = w_ukv[:, h * 192:h * 192 + 64]
        wv[:, h, :] = w_ukv[:, h * 192 + 64:h * 192 + 192]
    col = lambda g: np.ascontiguousarray(np.asarray(g, f).reshape(-1, 128).T)
    kaug = np.zeros((32, 1024), f)
    for c in range(16):
        kaug[c, c * 64:(c + 1) * 64] = 1.0
    shared = dict(
        kaug=kaug, ident=np.eye(128, dtype=f), w_kvr=w_kvr, w_main=w_main, g_in=col(norm_in_g),
        w_uqA=pk(wA.reshape(384, 1024)), w_uqB=pk(wB.reshape(384, 1024)), g_q=col(q_norm_g),
        w_k=pk(wk.reshape(256, 1024)), w_v=pk(wv.reshape(256, 1024)), g_kv=col(kv_norm_g),
        wsT=np.ascontiguousarray(np.asarray(w_spatial, f).transpose(2, 0, 1)),
        ln_b=np.asarray(gmlp_ln_b, f).reshape(1, 1024), bsp=np.asarray(b_spatial, f).reshape(1, 1024),
        lg=col(gmlp_ln_g), w_out=pk(np.asarray(w_out, f)),
        g_out=col(np.concatenate([np.asarray(out_norm_mla_g, f), np.asarray(out_norm_gmlp_g, f)])),
        g_fin=np.asarray(final_norm_g, f).reshape(1, 1024),
        ropek=np.stack([c2, s2], 0),
    )
    in_maps, rows_all = [], []
    for core in range(2 * B):
        b, p = core // 2, core % 2
        rows = np.concatenate([np.arange((8 * j + t) * 128, (8 * j + t + 1) * 128) for j in range(NSB) for t in TILES_P[p]])
        rows_all.append((b, rows))
        qa = np.zeros((32, 512), f)
        for i, t in enumerate(TILES_P[p]):
            for hh in range(2):
                qc = 2 * t + hh
                for c in range(16):
                    if qc < c:
                        qa[c, i * 128 + hh * 64:i * 128 + (hh + 1) * 64] = -BIG
        rq_ = np.stack([c2[:, rows], s2[:, rows]], 1)
        rq_ = np.ascontiguousarray(rq_.reshape(32, 2, NSB, 512).transpose(2, 0, 1, 3))
        m = dict(shared)
        m.update(xk=np.ascontiguousarray(x[b]), xq=np.ascontiguousarray(x[b][rows]), qaug=qa, ropeq=rq_)
        in_maps.append(m)
    return in_maps, rows_all


_NC_CACHE = {}


def kernel(**inputs):
    x = np.asarray(inputs["x"], np.float32)
    B, S, _ = x.shape
    NSB = S // 1024
    if NSB not in _NC_CACHE:
        _NC_CACHE[NSB] = build(NSB)
    nc = _NC_CACHE[NSB]
    in_maps, rows_all = make_inputs(NSB, **inputs)
    res = run_bass_kernel_spmd(nc, in_maps, core_ids=list(range(len(in_maps))))
    out = np.empty((B, S, D), np.float32)
    for core, (b, rows) in enumerate(rows_all):
        out[b, rows] = res.results[core]["out"]
    return out
```
